# Optimizing a Trainium2 kernel written in Bass

```python
import math
import jax
import jax.numpy as jnp
from jax import lax
import numpy as np

D_MODEL = 1024
BATCH = 4
SEQ = 8192
DEPTH = 1
DEC_BATCH = 8
DEC_SEQ = 32
PAST_LEN = 2048

CHUNK = 64
HEAD_DIM = 64
N_HEADS = D_MODEL // (2 * HEAD_DIM)
D_ATTN = N_HEADS * 2 * HEAD_DIM
D_POOL = D_MODEL
POOL_WINDOWS = (2, 4, 8, 16)
N_POOL_GROUPS = len(POOL_WINDOWS)
POOL_GROUP = D_POOL // N_POOL_GROUPS
POOL_HIST = max(POOL_WINDOWS) - 1
N_GATES = 2
D_IN = 3 * D_ATTN + D_POOL + N_GATES * D_MODEL
D_FF = ((8 * D_MODEL) // 3 + 255) // 256 * 256
NUM_BUCKETS = 32
MAX_DISTANCE = 128
Q_BLOCK = 128
EPS = 1e-6
SUBLN_EPS = 1e-5

kernel_name = "hybrid_diffattn_multiscale_pool_stream_step"


def _rmsnorm(x, g, eps=EPS):
    xf = x.astype(jnp.float32)
    y = xf * lax.rsqrt(jnp.mean(xf * xf, axis=-1, keepdims=True) + eps)
    return (y * g.astype(jnp.float32)).astype(x.dtype)


def _rel_bucket(rel):
    nb = NUM_BUCKETS // 2
    ret = jnp.where(rel > 0, nb, 0)
    n = jnp.abs(rel)
    max_exact = nb // 2
    large = max_exact + (jnp.log(jnp.maximum(n, 1).astype(jnp.float32) / max_exact)
                         / math.log(MAX_DISTANCE / max_exact) * (nb - max_exact)).astype(jnp.int32)
    large = jnp.minimum(large, nb - 1)
    return ret + jnp.where(n < max_exact, n, large)


def _lambda(layer_idx, lq1, lk1, lq2, lk2):
    lam_init = 0.8 - 0.6 * math.exp(-0.3 * layer_idx)
    f32 = jnp.float32
    lam = (jnp.exp(jnp.sum(lq1.astype(f32) * lk1.astype(f32)))
           - jnp.exp(jnp.sum(lq2.astype(f32) * lk2.astype(f32))) + lam_init)
    return lam, lam_init


def _diff_attn(q, k, v, q_pos, k_pos, rel_bias, lam):
    q1, q2 = q[..., :HEAD_DIM], q[..., HEAD_DIM:]
    k1, k2 = k[..., :HEAD_DIM], k[..., HEAD_DIM:]
    scale = HEAD_DIM ** -0.5
    bias = jnp.transpose(rel_bias[_rel_bucket(k_pos[None, :] - q_pos[:, None])],
                         (2, 0, 1)).astype(jnp.float32)
    mask = (k_pos[None, :] // CHUNK) <= (q_pos[:, None] // CHUNK)

    def probs(qa, ka):
        s = jnp.einsum('bqhd,bkhd->bhqk', qa, ka).astype(jnp.float32) * scale + bias
        s = jnp.where(mask, s, -jnp.inf)
        return jax.nn.softmax(s, axis=-1)

    a = probs(q1, k1) - lam * probs(q2, k2)
    return jnp.einsum('bhqk,bkhe->bqhe', a.astype(v.dtype), v)


def _prompt_attn(q, k, v, rel_bias, lam):
    b, t = q.shape[0], q.shape[1]
    k_pos = jnp.arange(t)

    def block(i):
        start = i * Q_BLOCK
        qb = lax.dynamic_slice_in_dim(q, start, Q_BLOCK, axis=1)
        q_pos = start + jnp.arange(Q_BLOCK)
        return _diff_attn(qb, k, v, q_pos, k_pos, rel_bias, lam)

    o = lax.map(block, jnp.arange(t // Q_BLOCK))
    return jnp.moveaxis(o, 0, 1).reshape(b, t, N_HEADS, 2 * HEAD_DIM)


def _attn_out(o, subln_g, lam_init):
    b, t = o.shape[0], o.shape[1]
    o = _rmsnorm(o, subln_g, SUBLN_EPS) * (1.0 - lam_init)
    return o.reshape(b, t, D_ATTN)


def _project(x, norm_mix, w_in, b_gate):
    b, t = x.shape[0], x.shape[1]
    h = _rmsnorm(x, norm_mix)
    z = h @ w_in
    hd = (b, t, N_HEADS, 2 * HEAD_DIM)
    q = z[..., :D_ATTN].reshape(hd)
    k = z[..., D_ATTN:2 * D_ATTN].reshape(hd)
    v = z[..., 2 * D_ATTN:3 * D_ATTN].reshape(hd)
    u = z[..., 3 * D_ATTN:3 * D_ATTN + D_POOL]
    g = jax.nn.sigmoid(z[..., 3 * D_ATTN + D_POOL:] + b_gate)
    return q, k, v, u, g[..., :D_MODEL], g[..., D_MODEL:]


def _pool_branch(u, hist, t0, w_pool, pool_scale):
    b, t, c = u.shape
    xcat = jnp.concatenate([hist, u], axis=1)
    cs0 = jnp.pad(jnp.cumsum(xcat.astype(jnp.float32), axis=1), ((0, 0), (1, 0), (0, 0)))
    uf = u.astype(jnp.float32)
    pos = t0 + jnp.arange(t)
    hi = POOL_HIST + 1
    outs = []
    for gi, w in enumerate(POOL_WINDOWS):
        sl = slice(gi * POOL_GROUP, (gi + 1) * POOL_GROUP)
        s = cs0[:, hi:hi + t, sl] - cs0[:, hi - w:hi - w + t, sl]
        cnt = jnp.minimum(pos + 1, w).astype(jnp.float32)[None, :, None]
        outs.append(s / cnt - uf[..., sl])
    d = jnp.stack(outs, axis=2).astype(u.dtype)
    y = jnp.einsum('btgc,gcd->btgd', d, w_pool).reshape(b, t, c)
    return y * pool_scale, xcat[:, -POOL_HIST:]


def _merge_ffn(x, attn, pool, g_a, g_p, w_out, norm_ffn, w_gate_up, w_down):
    x = x + (g_a * attn + g_p * pool) @ w_out
    h = _rmsnorm(x, norm_ffn)
    gu = h @ w_gate_up
    return x + (jax.nn.silu(gu[..., :D_FF]) * gu[..., D_FF:]) @ w_down


def setup_inputs(seed: int = 0) -> dict:
    key = jax.random.key(seed)
    ks = jax.random.split(key, 24)
    f32 = jnp.float32
    nrm = lambda k, s, sc: jax.random.normal(k, s, f32) * sc
    return {
        'x_prompt': nrm(ks[0], (BATCH, SEQ, D_MODEL), 1.0),
        'x_sample': nrm(ks[1], (DEC_BATCH, DEC_SEQ, D_MODEL), 1.0),
        'cache_k': nrm(ks[2], (DEPTH, DEC_BATCH, PAST_LEN, N_HEADS, 2 * HEAD_DIM), 1.0),
        'cache_v': nrm(ks[3], (DEPTH, DEC_BATCH, PAST_LEN, N_HEADS, 2 * HEAD_DIM), 1.0),
        'state_pool': nrm(ks[4], (DEPTH, DEC_BATCH, POOL_HIST, D_POOL), 1.0),
        'rel_bias': nrm(ks[5], (NUM_BUCKETS, N_HEADS), 0.5),
        'norm_mix': 1.0 + nrm(ks[6], (DEPTH, D_MODEL), 0.02),
        'w_in': nrm(ks[7], (DEPTH, D_MODEL, D_IN), D_MODEL ** -0.5),
        'b_gate': nrm(ks[8], (DEPTH, N_GATES * D_MODEL), 0.02),
        'lambda_q1': nrm(ks[9], (DEPTH, HEAD_DIM), 0.1),
        'lambda_k1': nrm(ks[10], (DEPTH, HEAD_DIM), 0.1),
        'lambda_q2': nrm(ks[11], (DEPTH, HEAD_DIM), 0.1),
        'lambda_k2': nrm(ks[12], (DEPTH, HEAD_DIM), 0.1),
        'subln_g': 1.0 + nrm(ks[13], (DEPTH, 2 * HEAD_DIM), 0.02),
        'w_pool': nrm(ks[14], (DEPTH, N_POOL_GROUPS, POOL_GROUP, POOL_GROUP), POOL_GROUP ** -0.5),
        'pool_scale': 1.0 + nrm(ks[15], (DEPTH, D_POOL), 0.1),
        'w_out': nrm(ks[16], (DEPTH, D_MODEL, D_MODEL), D_MODEL ** -0.5),
        'norm_ffn': 1.0 + nrm(ks[17], (DEPTH, D_MODEL), 0.02),
        'w_gate_up': nrm(ks[18], (DEPTH, D_MODEL, 2 * D_FF), D_MODEL ** -0.5),
        'w_down': nrm(ks[19], (DEPTH, D_FF, D_MODEL), D_FF ** -0.5),
        'norm_final': 1.0 + nrm(ks[20], (D_MODEL,), 0.02),
    }


def reference(x_prompt, x_sample, cache_k, cache_v, state_pool, rel_bias, norm_mix, w_in,
              b_gate, lambda_q1, lambda_k1, lambda_q2, lambda_k2, subln_g, w_pool,
              pool_scale, w_out, norm_ffn, w_gate_up, w_down, norm_final):
    xp, xs = x_prompt, x_sample
    past = cache_k.shape[2]
    s_new = x_sample.shape[1]
    kp_l, vp_l, pp_l, ks_l, vs_l, ps_l = [], [], [], [], [], []
    for l in range(DEPTH):
        lam, lam_init = _lambda(l, lambda_q1[l], lambda_k1[l], lambda_q2[l], lambda_k2[l])
        q, k, v, u, g_a, g_p = _project(xp, norm_mix[l], w_in[l], b_gate[l])
        attn = _attn_out(_prompt_attn(q, k, v, rel_bias, lam), subln_g[l], lam_init)
        zeros_hist = jnp.zeros((xp.shape[0], POOL_HIST, D_POOL), u.dtype)
        pool, pool_st = _pool_branch(u, zeros_hist, 0, w_pool[l], pool_scale[l])
        xp = _merge_ffn(xp, attn, pool, g_a, g_p, w_out[l], norm_ffn[l], w_gate_up[l], w_down[l])
        kp_l.append(k)
        vp_l.append(v)
        pp_l.append(pool_st)
        q, k, v, u, g_a, g_p = _project(xs, norm_mix[l], w_in[l], b_gate[l])
        k_all = jnp.concatenate([cache_k[l], k], axis=1)
        v_all = jnp.concatenate([cache_v[l], v], axis=1)
        q_pos = past + jnp.arange(s_new)
        k_pos = jnp.arange(past + s_new)
        attn = _attn_out(_diff_attn(q, k_all, v_all, q_pos, k_pos, rel_bias, lam),
                         subln_g[l], lam_init)
        pool, pool_st = _pool_branch(u, state_pool[l], past, w_pool[l], pool_scale[l])
        xs = _merge_ffn(xs, attn, pool, g_a, g_p, w_out[l], norm_ffn[l], w_gate_up[l], w_down[l])
        ks_l.append(k)
        vs_l.append(v)
        ps_l.append(pool_st)
    y_prompt = _rmsnorm(xp, norm_final)
    y_sample = _rmsnorm(xs, norm_final)
    k_prompt = jnp.stack(kp_l)
    v_prompt = jnp.stack(vp_l)
    pool_prompt = jnp.stack(pp_l)
    k_sample = jnp.stack(ks_l)
    v_sample = jnp.stack(vs_l)
    pool_sample = jnp.stack(ps_l)
    return (y_prompt, y_sample, k_prompt, v_prompt, pool_prompt, k_sample, v_sample, pool_sample)
```

```python
import math
from contextlib import ExitStack
import numpy as np
import ml_dtypes
import concourse.bass as bass
import concourse.mybir as mybir
from concourse.bass_utils import run_bass_kernel_spmd

F32 = mybir.dt.float32
BF16 = mybir.dt.bfloat16
ALU = mybir.AluOpType
AF = mybir.ActivationFunctionType
AP = bass.AP

D = 1024
NT = 512
NSLOT = 8
H = 8
KC = 8
DFF = 2816
NFC = 22
SEQ = 8192
PAST = 2048
NS = 32
NEG = -30000.0
EPS = 1e-6
SUBLN_EPS = 1e-5
LAM_INIT = 0.8 - 0.6 * math.exp(-0.3 * 0)
KT_S = SEQ
KT_N = SEQ + PAST
KTCOLS = SEQ + PAST + 512
VB_S = 64
VB_N = 80
VBLKS = 81
STAGE = 99
STQ = "act"
NFE = 4


class Res:
    __slots__ = ("name", "w", "rs", "sem", "cnt")

    def __init__(self, name):
        self.name = name
        self.w = None
        self.rs = []
        self.sem = {}
        self.cnt = {"hw": 0, "sw": 0}


class Op:
    __slots__ = ("eng", "fn", "deps", "dma", "res", "val", "needed", "tick", "key")

    def __init__(self, eng, fn, dma):
        self.eng = eng
        self.fn = fn
        self.dma = dma
        self.deps = []
        self.res = None
        self.val = 0
        self.needed = False
        self.tick = 0


ENGS = ("pe", "act", "dve", "pool", "sp")
QK = {"sp": 16, "act": 8, "pool": 8, "pe": 1, "dve": 1}


class Sched:
    def __init__(self):
        self.ops = {e: [] for e in ENGS}
        self.all = []
        self.bar = []
        self.dma_since = []
        self.dmares = []
        self.qs = {e: [] for e in ENGS}
        self.qn = {e: 0 for e in ENGS}

    def add(self, eng, fn, reads=(), writes=(), dma=False, nowaw=False, extra=()):
        op = Op(eng, fn, dma)
        deps = set(self.bar)
        deps.update(extra)
        for r in reads:
            if r.w is not None:
                deps.add(r.w)
        for r in writes:
            if r.w is not None and not nowaw:
                deps.add(r.w)
            for q in r.rs:
                deps.add(q)
        for r in reads:
            r.rs.append(op)
        for r in writes:
            if nowaw:
                r.w = op
            else:
                r.w = op
                r.rs = []
        if dma:
            K = QK[eng]
            lst = self.qs[eng]
            i = self.qn[eng] % K
            self.qn[eng] += 1
            if len(lst) <= i:
                lst.append({"cnt": 0, "last": None})
            ent = lst[i]
            if ent["last"] is not None:
                deps.add(ent["last"])
            ent["cnt"] += 16
            ent["last"] = op
            op.key = (eng, i)
            op.val = ent["cnt"]
            self.dma_since.append(op)
        deps.discard(op)
        op.deps = list(deps)
        for d in op.deps:
            d.needed = True
        self.ops[eng].append(op)
        self.all.append(op)
        return op

    def barrier(self):
        b = []
        for e in ENGS:
            if self.ops[e]:
                b.append(self.ops[e][-1])
        b.extend(self.dma_since)
        self.dma_since = []
        self.bar = b


def _bucket_consts():
    import jax
    import jax.numpy as jnp
    cpu = jax.devices("cpu")[0]
    with jax.default_device(cpu):
        rel = jnp.asarray(127 - np.arange(383), dtype=jnp.int32)
        nb = 16
        ret = jnp.where(rel > 0, nb, 0)
        n = jnp.abs(rel)
        max_exact = nb // 2
        large = max_exact + (jnp.log(jnp.maximum(n, 1).astype(jnp.float32) / max_exact)
                             / math.log(128 / max_exact) * (nb - max_exact)).astype(jnp.int32)
        large = jnp.minimum(large, nb - 1)
        bk = np.asarray(ret + jnp.where(n < max_exact, n, large))
    ew = np.zeros((32, 383), np.float32)
    ew[bk, np.arange(383)] += 1.0
    ew[15, :] -= 1.0
    return ew


def build(stage=STAGE):
    nc = bass.Bass("TRN2", target_bir_lowering=False)

    def din(name, shape, dt=F32):
        return nc.dram_tensor(name, list(shape), dt, kind="ExternalInput").ap()

    def dout(name, shape, dt=F32):
        return nc.dram_tensor(name, list(shape), dt, kind="ExternalOutput").ap()

    def dscr(name, shape, dt):
        return nc.dram_tensor(name, list(shape), dt).ap()

    x_own = din("x_own", [NSLOT, NT, D])
    x_for = din("x_for", [NSLOT, NT, D])
    x_prev = din("x_prev", [NSLOT, 16, D])
    wsel = din("wsel", [128, 4])
    invcnt = din("invcnt", [NSLOT + 1, 4, NT])
    xs = din("xs", [NS, D])
    hist_s = din("hist_s", [16, D])
    ck = din("ck", [PAST, D])
    cv = din("cv", [PAST, D])
    w_in = din("w_in", [D, 6 * D])
    b_gate = din("b_gate", [128, 16])
    w_pool = din("w_pool", [4, 256, 256])
    pool_scale = din("pool_scale", [128, 8])
    w_out = din("w_out", [D, D])
    w_gu = din("w_gu", [D, 2 * DFF])
    w_down = din("w_down", [DFF, D])
    norm_mix = din("norm_mix", [1, D])
    norm_ffn = din("norm_ffn", [1, D])
    norm_final = din("norm_final", [1, D])
    subln_g = din("subln_g", [128, 1])
    lamv = din("lamv", [1, 256])
    rel_bias = din("rel_bias", [32, 8])
    ident_d = din("ident", [128, 128], BF16)
    identf_d = din("identf", [128, 128])
    ewin_d = din("ewin", [32, 383])
    jf_d = din("jf", [128, 128])
    maskt_d = din("maskt", [128, 256])

    y_own = dout("y_own", [NSLOT, NT, D])
    k_own = dout("k_own", [NSLOT, NT, D])
    v_own = dout("v_own", [NSLOT, NT, D])
    pool_tail = dout("pool_tail", [15, D])
    y_s = dout("y_s", [NS, D])
    k_s = dout("k_s", [NS, D])
    v_s = dout("v_s", [NS, D])
    pool_s = dout("pool_s", [15, D])

    WinS = dscr("WinS", [12, 128, KC, 512], BF16)
    WoutS = dscr("WoutS", [2, 128, KC, 512], BF16)
    WguS = dscr("WguS", [22, 128, KC, 256], BF16)
    WdnS = dscr("WdnS", [11, 128, 2, D], BF16)
    KTs = dscr("KTs", [H, 128, KTCOLS], BF16)
    Vs = dscr("Vs", [H, 128, VBLKS, 128], BF16)
    gtab = dscr("gtab", [8, 383], F32)
    KTs_v = KTs.rearrange("h d c -> d h c")
    Vs_v = Vs.rearrange("h p b e -> p h b e")

    S = Sched()
    es = ExitStack()
    with es:
        def sb(name, shape, dt=F32):
            return es.enter_context(nc.sbuf_tensor("sb_" + name, list(shape), dt))

        R = Res

        identb = sb("identb", [128, 128], BF16)
        identf = sb("identf", [128, 128])
        onesb = sb("onesb", [128, 128], BF16)
        onesf = sb("onesf", [128, 128])
        gmix = sb("gmix", [128, D])
        gffn = sb("gffn", [128, D])
        gfin = sb("gfin", [128, D])
        bgate = sb("bgate", [128, 16])
        pscale = sb("pscale", [128, 8])
        sg08 = sb("sg08", [128, 1])
        lamt = sb("lamt", [128, 256])
        lamj = sb("lamj", [128, 64])
        lam2 = sb("lam2", [128, 2])
        nlam = sb("nlam", [128, 1])
        cb = sb("cb", [128, 8])
        cbm = sb("cbm", [128, 2, 8])
        negt = sb("negt", [128, 8])
        wselt = sb("wselt", [128, 4])
        Dp = sb("Dp", [128, 8, 256])
        wpool = sb("wpool", [128, 4, 2, 256], BF16)
        epsln = sb("epsln", [128, 1])
        epsmx = sb("epsmx", [128, 1])
        constR = R("const")
        setupR = R("setup")

        xt = [sb(f"xt{i}", [128, D]) for i in range(NFE)]
        xtR = [R(f"xt{i}") for i in range(NFE)]
        hb = [sb(f"hb{i}", [128, D], BF16) for i in range(NFE)]
        hbR = [R(f"hb{i}") for i in range(NFE)]
        ss = [sb(f"ss{i}", [128, 1]) for i in range(NFE)]
        ssR = [R(f"ss{i}") for i in range(NFE)]
        rs = [sb(f"rs{i}", [128, 1]) for i in range(NFE)]
        rsR = [R(f"rs{i}") for i in range(NFE)]
        hTs = [sb(f"hT{i}", [128, KC, NT], BF16) for i in range(2)]
        hTsR = [R(f"hT{i}") for i in range(2)]
        hTp = sb("hTp", [128, KC, 16], BF16)
        hTpR = R("hTp")
        NW = 3
        wst = [sb(f"wst{i}", [128, KC, 512], BF16) for i in range(NW)]
        wstR = [R(f"wst{i}") for i in range(NW)]
        KTst = sb("KTst", [128, H, NT], BF16)
        KTstR = R("KTst")
        Vst = sb("Vst", [128, H, 4, 128], BF16)
        VstR = R("Vst")
        kvo = [sb(f"kvo{i}", [128, 512]) for i in range(2)]
        kvoR = [R(f"kvo{i}") for i in range(2)]
        QT = sb("QT", [128, H, NT], BF16)
        QTR = R("QT")
        PLT = sb("PLT", [128, 8, NT], BF16)
        PLTR = R("PLT")
        MT = sb("MT", [128, 8, NT], BF16)
        MTR = R("MT")
        ARENA_N = 15360
        arena_t = sb("arena", [128, ARENA_N])

        class Arena:
            def __init__(self):
                self.off = 0

            def reset(self):
                self.off = 0

            def take(self, shape, dt=F32):
                n = 1
                for q in shape:
                    n *= q
                nf = n if dt == F32 else (n + 1) // 2
                nf = (nf + 1) // 2 * 2
                assert self.off + nf <= ARENA_N, (self.off, nf)
                a = arena_t[:, self.off:self.off + nf]
                self.off += nf
                if dt != F32:
                    a = a.bitcast(dt)
                a = a[:, 0:n]
                if len(shape) == 2:
                    a = a.rearrange("p (a b) -> p a b", a=shape[0])
                elif len(shape) == 3:
                    a = a.rearrange("p (a b c) -> p a b c", a=shape[0], b=shape[1])
                return a

        arena = Arena()

        PS = [es.enter_context(nc.psum_tensor(f"ps{i}", [128, 2, 512], F32)) for i in range(4)]
        PSR = [[R(f"ps{i}a"), R(f"ps{i}b")] for i in range(4)]
        bank_rr = [0]

        def bank_k(k):
            return PS[k // 2][:, k % 2, :], PSR[k // 2][k % 2]

        def next_bank():
            k = bank_rr[0] % 8
            bank_rr[0] += 1
            return bank_k(k)

        outR = R("outputs")
        invcR_g, hsR_g, ckbR_g = R("invc"), R("hs"), R("ckb")
        KTcR_g = [R(f"KTc{i}") for i in range(4)]
        VcR_g = [R(f"Vc{i}") for i in range(4)]
        x1R_g = [R(f"x1_{i}") for i in range(4)]
        wscrR = R("wscr")
        kvR = [R(f"kvpos{p}") for p in range(17)]
        gtabR = R("gtab")

        pool_dmas = []

        def DMA(q, out, in_, reads, writes, nowaw=False):
            def fn(e):
                return e.dma_start(out=out, in_=in_)
            extra = ()
            if q == "pool" and len(pool_dmas) >= 3:
                extra = (pool_dmas[-3],)
            op = S.add(q, fn, reads, writes, dma=True, nowaw=nowaw, extra=extra)
            if q == "pool":
                pool_dmas.append(op)
            return op

        def MM(out, pairs, reads, writes):
            def fn(e):
                n = len(pairs)
                ins = None
                for i, (l, r) in enumerate(pairs):
                    ins = e.matmul(out, lhsT=l, rhs=r, start=(i == 0), stop=(i == n - 1))
                return ins
            return S.add("pe", fn, reads, writes)

        def MM1(out, l, r, start, stop, reads, writes, skip=False):
            def fn(e):
                if skip:
                    return e.matmul(out, lhsT=l, rhs=r, start=start, stop=stop, skip_group_check=True)
                return e.matmul(out, lhsT=l, rhs=r, start=start, stop=stop)
            return S.add("pe", fn, reads, writes)

        def TRS(items, ident, reads, writes):
            def fn(e):
                ins = None
                for (o, i) in items:
                    ins = e.transpose(o, i, ident)
                return ins
            return S.add("pe", fn, reads, writes)

        def ACTV(out, in_, func, reads, writes, bias=None, scale=None):
            def fn(e):
                kw = {}
                if bias is not None:
                    kw["bias"] = bias
                if scale is not None:
                    kw["scale"] = scale
                return e.activation(out=out, in_=in_, func=func, **kw)
            return S.add("act", fn, reads, writes)

        def VCOPY(out, in_, reads, writes, eng="dve"):
            def fn(e):
                return e.tensor_copy(out=out, in_=in_)
            return S.add(eng, fn, reads, writes)

        def VTT(out, in0, in1, op, reads, writes, eng="dve"):
            def fn(e):
                return e.tensor_tensor(out=out, in0=in0, in1=in1, op=op)
            return S.add(eng, fn, reads, writes)

        def VTS(out, in0, s1, op0, reads, writes, eng="dve"):
            def fn(e):
                return e.tensor_scalar(out=out, in0=in0, scalar1=s1, scalar2=None, op0=op0)
            return S.add(eng, fn, reads, writes)

        def VSTT(out, in0, scalar, in1, op0, op1, reads, writes, accum=None):
            def fn(e):
                if accum is not None:
                    return e.scalar_tensor_tensor(out=out, in0=in0, scalar=scalar, in1=in1,
                                                  op0=op0, op1=op1, accum_out=accum)
                return e.scalar_tensor_tensor(out=out, in0=in0, scalar=scalar, in1=in1,
                                              op0=op0, op1=op1)
            return S.add("dve", fn, reads, writes)

        def VRECIP(out, in_, reads, writes):
            def fn(e):
                return e.reciprocal(out=out, in_=in_)
            return S.add("dve", fn, reads, writes)

        def MEMSET(t, val, writes, eng="pool"):
            def fn(e):
                return e.memset(t, val)
            return S.add(eng, fn, (), writes)

        def bcast_rows(ap2d, nparts):
            n = ap2d.shape[-1]
            return AP(ap2d.tensor, ap2d.offset, [[0, nparts], [1, n]])

        def bcast_mid(a, m):
            apl = a.ap
            return AP(a.tensor, a.offset, [list(apl[0]), [0, m], list(apl[-1])])

        for (t, src) in ((identb[:], ident_d), (identf[:], identf_d),
                         (gmix[:], bcast_rows(norm_mix, 128)), (gffn[:], bcast_rows(norm_ffn, 128)),
                         (gfin[:], bcast_rows(norm_final, 128)), (bgate[:], b_gate),
                         (pscale[:], pool_scale), (sg08[:], subln_g),
                         (lamt[:], bcast_rows(lamv, 128)), (wselt[:], wsel),
                         (cb[:], bcast_rows(rel_bias[15:16, :], 128))):
            DMA("sp", t, src, (), [constR], nowaw=True)
        DMA("pool", wpool[:], w_pool.rearrange("g (cc p) d -> p g cc d", p=128), (), [constR], nowaw=True)
        MEMSET(onesb[:], 1.0, [setupR])
        MEMSET(onesf[:], 1.0, [setupR])
        MEMSET(epsln[:], SUBLN_EPS, [setupR])
        MEMSET(epsmx[:], EPS, [setupR])
        MEMSET(negt[:], NEG, [setupR])
        if True:
            arena.reset()
            ewin = arena.take([384])[0:32, 0:383]
            rbt = arena.take([8])[0:32, :]
            jt = arena.take([128])
            maskt = arena.take([256])
            gsb = arena.take([384])[0:8, 0:383]
            Hk = arena.take([8, 256])
            c2R = R("const2")
            for (t, src) in ((ewin, ewin_d), (rbt, rel_bias), (jt, jf_d), (maskt, maskt_d)):
                DMA("sp", t, src, (), [c2R], nowaw=True)
            S.barrier()
            VTS(sg08[:], sg08[:], 1.0 - LAM_INIT, ALU.mult, [setupR], [setupR])
            VSTT(lamj[:], lamt[:, 0:64], 1.0, lamt[:, 64:128], ALU.mult, ALU.mult, [setupR], [setupR],
                 accum=lam2[:, 0:1])
            VSTT(lamj[:], lamt[:, 128:192], 1.0, lamt[:, 192:256], ALU.mult, ALU.mult, [setupR], [setupR],
                 accum=lam2[:, 1:2])
            ACTV(lam2[:], lam2[:], AF.Exp, [setupR], [setupR])
            VTT(nlam[:], lam2[:, 1:2], lam2[:, 0:1], ALU.subtract, [setupR], [setupR])
            VTS(nlam[:], nlam[:], -LAM_INIT, ALU.add, [setupR], [setupR])
            for par in range(2):
                VSTT(cbm[:, par, :], negt[:], wselt[:, 2 * par:2 * par + 1], cb[:], ALU.mult, ALU.add,
                     [setupR], [setupR])
            bk, bkR = next_bank()
            MM(bk[0:8, 0:383], [(rbt, ewin)], [setupR], [bkR])
            VCOPY(gsb, bk[0:8, 0:383], [bkR], [setupR])
            DMA("sp", gtab, gsb, [setupR], [gtabR])
            hkR = R("Hk")
            DMA("sp", Hk, AP(gtab.tensor, gtab.offset, [[1, 128], [383, 8], [1, 256]]), [gtabR], [hkR])
            for hp in range(4):
                bk, bkR = next_bank()
                for j in range(2):
                    MM(bk[:, j * 256:(j + 1) * 256], [(jt, Hk[:, 2 * hp + j, :])], [hkR, setupR], [bkR])
                VTT(Dp[:, 2 * hp:2 * hp + 2, :], bk.rearrange("p (a b) -> p a b", a=2),
                    bcast_mid(maskt, 2), ALU.add, [bkR, setupR], [setupR])
            S.barrier()

        wrr = [0]

        arena.reset()
        cst = [arena.take([KC, 512], BF16) for _ in range(2)]
        cstR = [R("cst0"), R("cst1")]
        crr_ = [0]
        wrr = [0]
        scrR = {}

        def conv(src, dst, key, view=None):
            sl = crr_[0] % 2
            crr_[0] += 1
            tl = cst[sl] if view is None else view(cst[sl])
            scrR[key] = R("scr")
            DMA("pool", tl, src, (), [cstR[sl]])
            DMA(STQ, dst, tl, [cstR[sl]], [scrR[key]])

        w_in_v = w_in.rearrange("(kc p) c -> p kc c", p=128)
        w_out_v = w_out.rearrange("(kc p) c -> p kc c", p=128)
        w_gu_v = w_gu.rearrange("(kc p) c -> p kc c", p=128)
        w_dn_v = w_down.rearrange("(fc p) c -> p fc c", p=128)

        def v256(t):
            return t[:, :, 0:256]

        def vdn(t):
            return t[:, :, :].rearrange("p a b -> p (a b)")[:, 0:2048].rearrange("p (a b) -> p a b", a=2)

        wkv = [arena.take([KC, 512], BF16) for _ in range(4)]
        wkvR = [R(f"wkv{i}") for i in range(4)]
        if stage >= 0:
            for idx, g in enumerate((2, 3, 4, 5)):
                DMA("pool", wkv[idx], w_in_v[:, :, g * 512:(g + 1) * 512], (), [wkvR[idx]])
                scrR[("in", g)] = R("scr")
                DMA(STQ, WinS[g], wkv[idx], [wkvR[idx]], [scrR[("in", g)]])
        conv_list = []
        late1, late2 = [], []
        for g in (6, 7, 0, 1):
            conv_list.append((w_in_v[:, :, g * 512:(g + 1) * 512], WinS[g], ("in", g), None))
        for g in (8, 9, 10, 11):
            late1.append((w_in_v[:, :, g * 512:(g + 1) * 512], WinS[g], ("in", g), None))
        for g in range(2):
            late1.append((w_out_v[:, :, g * 512:(g + 1) * 512], WoutS[g], ("out", g), None))
        for g in range(11):
            late2.append((w_gu_v[:, :, g * 256:(g + 1) * 256], WguS[g], ("gu", g), v256))
            late2.append((w_gu_v[:, :, (11 + g) * 256:(12 + g) * 256], WguS[11 + g], ("gu", 11 + g), v256))
        for g in range(11):
            late2.append((w_dn_v[:, 2 * g:2 * g + 2, :], WdnS[g], ("dn", g), vdn))
        conv_pos = [0]

        def conv_late(lst):
            for (src, dst, key, view) in lst:
                sl = wrr[0] % NW
                wrr[0] += 1
                tl = wst[sl][:] if view is None else view(wst[sl])
                scrR[key] = R("scr")
                DMA("pool", tl, src, (), [wstR[sl]])
                DMA("sp", dst, tl, [wstR[sl]], [scrR[key]])
            del lst[:]

        def conv_some(k):
            for _ in range(k):
                if conv_pos[0] < len(conv_list):
                    a, b, c, d = conv_list[conv_pos[0]]
                    conv_pos[0] += 1
                    conv(a, b, c, d)


        pfw = {}

        def load_w(src, key, view=None):
            if key in pfw:
                return pfw.pop(key)
            sl = wrr[0] % NW
            wrr[0] += 1
            tl = wst[sl][:] if view is None else view(wst[sl])
            DMA("sp", tl, src, [scrR[key]], [wstR[sl]])
            return wst[sl], wstR[sl]

        def prefetch_w(src, key, view=None):
            pfw[key] = load_w(src, key, view)

        fe_rr = [0]

        def frontend_batch(items):
            st = []
            for it in items:
                sl = fe_rr[0] % NFE
                fe_rr[0] += 1
                n = it["n"]
                if it.get("xsrc") is not None:
                    DMA("sp", xt[sl][0:n, :], it["xsrc"], (), [xtR[sl]])
                    xa, xR = xt[sl][0:n, :], xtR[sl]
                else:
                    xa, xR = it["xtile"], it["xtileR"]
                st.append((sl, n, xa, xR, it))
            for (sl, n, xa, xR, it) in st:
                VSTT(hb[sl][0:n, :], xa, 1.0, xa, ALU.mult, ALU.mult, [xR], [hbR[sl], ssR[sl]],
                     accum=ss[sl][0:n, :])
            for (sl, n, xa, xR, it) in st:
                ACTV(rs[sl][0:n, :], ss[sl][0:n, :], AF.Ln, [ssR[sl]], [rsR[sl]],
                     bias=epsmx[0:n, :], scale=1.0 / D)
            for (sl, n, xa, xR, it) in st:
                ACTV(rs[sl][0:n, :], rs[sl][0:n, :], AF.Exp, [rsR[sl]], [rsR[sl]], scale=-0.5)
            for (sl, n, xa, xR, it) in st:
                VSTT(hb[sl][0:n, :], xa, rs[sl][0:n, 0:1], it["gain"][0:n, :], ALU.mult, ALU.mult,
                     [xR, rsR[sl], hbR[sl]], [hbR[sl]])
            bks = []
            for (sl, n, xa, xR, it) in st:
                bk, bkR = next_bank()
                pb = bk.bitcast(BF16)
                TRS([(pb[:, kc * 128:kc * 128 + n], hb[sl][0:n, kc * 128:(kc + 1) * 128]) for kc in range(KC)],
                    identb[0:n, 0:n], [hbR[sl]], [bkR])
                bks.append((pb, bkR))
            for (sl, n, xa, xR, it), (pb, bkR) in zip(st, bks):
                VCOPY(it["dst"], pb.rearrange("p (k t) -> p k t", k=KC)[:, :, 0:n], [bkR], [it["dstR"]])

        def frontend(n, gain, dst, dstR, xsrc=None, xtile=None, xtileR=None):
            frontend_batch([dict(n=n, gain=gain, dst=dst, dstR=dstR, xsrc=xsrc, xtile=xtile, xtileR=xtileR)])

        kvo_rr = [0]

        def kvo_slot():
            sl = kvo_rr[0] % 2
            kvo_rr[0] += 1
            return kvo[sl], kvoR[sl]

        hT_rr = [0]

        def next_hT():
            sl = hT_rr[0] % 2
            hT_rr[0] += 1
            return hTs[sl], hTsR[sl]

        def kt_fm(hTc, hTcR, n, wt, wR, hbase):
            for ci in range(4):
                bk, bkR = next_bank()
                MM(bk[:, 0:n], [(wt[:, kc, ci * 128:(ci + 1) * 128], hTc[:, kc, 0:n]) for kc in range(KC)],
                   [wR, hTcR], [bkR])
                ACTV(KTst[:, hbase + ci, 0:n], bk[:, 0:n], AF.Copy, [bkR], [KTstR])

        def phaseA_front(f):
            hTc, hTcR = next_hT()
            frontend_batch([dict(n=128, gain=gmix, dst=hTc[:, :, blk * 128:(blk + 1) * 128], dstR=hTcR,
                                 xsrc=x_for[f, blk * 128:(blk + 1) * 128, :]) for blk in range(4)])
            return hTc, hTcR

        def phaseA_mm(f, hTc, hTcR):
            pos = 2 * f + 1
            for gi in (2, 3):
                wt, wR = wkv[gi - 2], wkvR[gi - 2]
                kt_fm(hTc, hTcR, NT, wt, wR, (gi - 2) * 4)
            DMA(STQ, KTs_v[:, :, pos * 512:(pos + 1) * 512], KTst[:], [KTstR], [kvR[pos]], nowaw=True)
            for gi in (4, 5):
                wt, wR = wkv[gi - 2], wkvR[gi - 2]
                for blk in range(4):
                    bk, bkR = next_bank()
                    MM(bk[:, :], [(hTc[:, kc, blk * 128:(blk + 1) * 128], wt[:, kc, :]) for kc in range(KC)],
                       [wR, hTcR], [bkR])
                    VCOPY(Vst[:, (gi - 4) * 4:(gi - 4) * 4 + 4, blk, :],
                          bk.rearrange("p (h e) -> p h e", h=4), [bkR], [VstR])
            DMA(STQ, Vs_v[:, :, pos * 4:(pos + 1) * 4, :], Vst[:], [VstR], [kvR[pos]], nowaw=True)

        NSL = NSLOT
        if stage >= 1:
            cur = phaseA_front(0)
            for f in range(NSL):
                nxt = phaseA_front(f + 1) if f + 1 < NSL else None
                conv_some(6)
                phaseA_mm(f, cur[0], cur[1])
                cur = nxt
        conv_some(len(conv_list))
        S.barrier()

        def phaseB1_front(cfg):
            n, bs, nblk = cfg["n"], cfg["bs"], cfg["nblk"]
            hTc, hTcR = next_hT()
            cfg["hT"] = (hTc, hTcR)
            items = [dict(n=bs, gain=gmix, dst=hTc[:, :, blk * bs:(blk + 1) * bs], dstR=hTcR,
                          xsrc=cfg["x"][blk * bs:(blk + 1) * bs, :]) for blk in range(nblk)]
            frontend_batch(items)
            if cfg["prev"] is not None:
                frontend(16, gmix, hTp[:, :, :], hTpR, xsrc=cfg["prev"])

        def phaseB1(cfg):
            n, bs, nblk = cfg["n"], cfg["bs"], cfg["nblk"]
            if stage >= 2:
                conv_late(late1)
            arena.reset()
            UT = arena.take([8, 16 + NT])
            invc = arena.take([4, NT])
            a1 = arena.take([2, 16 + NT])
            a2 = arena.take([2, 16 + NT])
            a3 = arena.take([2, 16 + NT])
            DT = arena.take([8, NT], BF16)
            UTR, invcR, aR, DTR = R("UT"), invcR_g, R("apool"), R("DT")
            if "hT" not in cfg:
                phaseB1_front(cfg)
            hTc, hTcR = cfg["hT"]
            DMA("sp", invc, AP(invcnt.tensor, invcnt[cfg["slot"]].offset, [[0, 128], [NT, 4], [1, NT]]),
                (), [invcR])
            if cfg["prev"] is None:
                hs = arena.take([D])
                hsR = hsR_g
                DMA("sp", hs[0:16, :], hist_s, (), [hsR])
                bk, bkR = next_bank()
                TRS([(bk[:, kc * 16:(kc + 1) * 16], hs[0:16, kc * 128:(kc + 1) * 128]) for kc in range(KC)],
                    identf[0:16, 0:16], [hsR], [bkR])
                VCOPY(UT[:, :, 0:16], bk[:, 0:128].rearrange("p (k t) -> p k t", k=KC), [bkR], [UTR])
            for gi in (6, 7):
                wt, wR = load_w(WinS[gi], ("in", gi))
                for ci in range(4):
                    c = (gi - 6) * 4 + ci
                    bk, bkR = next_bank()
                    MM(bk[:, 0:n], [(wt[:, kc, ci * 128:(ci + 1) * 128], hTc[:, kc, 0:n]) for kc in range(KC)],
                       [wR, hTcR], [bkR])
                    VCOPY(UT[:, c, 16:16 + n], bk[:, 0:n], [bkR], [UTR])
                    if cfg["prev"] is not None:
                        bk, bkR = next_bank()
                        MM(bk[:, 0:16], [(wt[:, kc, ci * 128:(ci + 1) * 128], hTp[:, kc, :]) for kc in range(KC)],
                           [wR, hTpR], [bkR])
                        VCOPY(UT[:, c, 0:16], bk[:, 0:16], [bkR], [UTR])
                if cfg["tail"] is not None:
                    tdst, row0 = cfg["tail"]
                    blk = nblk - 1
                    bk, bkR = next_bank()
                    MM(bk[0:bs, :], [(hTc[:, kc, blk * bs:(blk + 1) * bs], wt[:, kc, :]) for kc in range(KC)],
                       [wR, hTcR], [bkR])
                    ko, koR = kvo_slot()
                    VCOPY(ko[0:bs, :], bk[0:bs, :], [bkR], [koR])
                    DMA(STQ, tdst[:, (gi - 6) * 512:(gi - 5) * 512], ko[row0:row0 + 15, :], [koR], [outR],
                        nowaw=True)
            L = 16 + n
            for g in range(4):
                c0 = 2 * g
                U2 = UT[:, c0:c0 + 2, :]
                VTT(a1[:, :, 1:L], U2[:, :, 1:L], U2[:, :, 0:L - 1], ALU.add, [UTR], [aR])
                cur = a1
                if g >= 1:
                    VTT(a2[:, :, 3:L], a1[:, :, 3:L], a1[:, :, 1:L - 2], ALU.add, [aR], [aR])
                    cur = a2
                if g >= 2:
                    VTT(a3[:, :, 7:L], a2[:, :, 7:L], a2[:, :, 3:L - 4], ALU.add, [aR], [aR])
                    cur = a3
                if g >= 3:
                    VTT(a1[:, :, 15:L], a3[:, :, 15:L], a3[:, :, 7:L - 8], ALU.add, [aR], [aR])
                    cur = a1
                oth = a2 if cur is not a2 else a3
                VTT(oth[:, :, 16:L], cur[:, :, 16:L], bcast_mid(invc[:, g, 0:n], 2), ALU.mult,
                    [aR, invcR], [aR])
                VTT(DT[:, c0:c0 + 2, 0:n], oth[:, :, 16:L], U2[:, :, 16:L], ALU.subtract, [aR, UTR], [DTR])
            for gi in (0, 1):
                wt, wR = load_w(WinS[gi], ("in", gi))
                for ci in range(4):
                    bk, bkR = next_bank()
                    MM(bk[:, 0:n], [(wt[:, kc, ci * 128:(ci + 1) * 128], hTc[:, kc, 0:n]) for kc in range(KC)],
                       [wR, hTcR], [bkR])
                    ACTV(QT[:, gi * 4 + ci, 0:n], bk[:, 0:n], AF.Copy, [bkR], [QTR], scale=0.125)
            for gi in (2, 3):
                wt, wR = load_w(WinS[gi], ("in", gi))
                kt_fm(hTc, hTcR, n, wt, wR, (gi - 2) * 4)
                for blk in range(nblk):
                    bk, bkR = next_bank()
                    MM(bk[0:bs, :], [(hTc[:, kc, blk * bs:(blk + 1) * bs], wt[:, kc, :]) for kc in range(KC)],
                       [wR, hTcR], [bkR])
                    ko, koR = kvo_slot()
                    VCOPY(ko[0:bs, :], bk[0:bs, :], [bkR], [koR])
                    DMA(STQ, cfg["k_out"][blk * bs:(blk + 1) * bs, (gi - 2) * 512:(gi - 1) * 512],
                        ko[0:bs, :], [koR], [outR], nowaw=True)
            DMA(STQ, KTs_v[:, :, cfg["kcol"]:cfg["kcol"] + n], KTst[:, :, 0:n], [KTstR],
                [kvR[cfg["pos"]]], nowaw=True)
            for gi in (4, 5):
                wt, wR = load_w(WinS[gi], ("in", gi))
                for blk in range(nblk):
                    bk, bkR = next_bank()
                    MM(bk[0:bs, :], [(hTc[:, kc, blk * bs:(blk + 1) * bs], wt[:, kc, :]) for kc in range(KC)],
                       [wR, hTcR], [bkR])
                    ko, koR = kvo_slot()
                    VCOPY(ko[0:bs, :], bk[0:bs, :], [bkR], [koR])
                    VCOPY(Vst[0:bs, (gi - 4) * 4:(gi - 4) * 4 + 4, blk, :],
                          ko[0:bs, :].rearrange("p (h e) -> p h e", h=4), [koR], [VstR], eng="pool")
                    DMA(STQ, cfg["v_out"][blk * bs:(blk + 1) * bs, (gi - 4) * 512:(gi - 3) * 512],
                        ko[0:bs, :], [koR], [outR], nowaw=True)
            DMA(STQ, Vs_v[0:bs, :, cfg["vblk"]:cfg["vblk"] + nblk, :], Vst[0:bs, :, 0:nblk, :], [VstR],
                [kvR[cfg["pos"]]], nowaw=True)
            for g in range(4):
                for oc in range(2):
                    bk, bkR = next_bank()
                    MM(bk[:, 0:n], [(wpool[:, g, cc, oc * 128:(oc + 1) * 128], DT[:, 2 * g + cc, 0:n])
                                    for cc in range(2)], [DTR], [bkR])
                    c = 2 * g + oc
                    VTS(PLT[:, c, 0:n], bk[:, 0:n], pscale[:, c:c + 1], ALU.mult, [bkR], [PLTR])
            S.barrier()

        def phaseB2(cfg, blocks):
            n = cfg["n"]
            conv_late(late2)
            arena.reset()
            NCH = 4
            KTc = [arena.take([2048], BF16) for _ in range(NCH)]
            Vc = [arena.take([16, 128], BF16) for _ in range(NCH)]
            KTcR = KTcR_g
            VcR = VcR_g
            NPT = 4
            PT = [arena.take([2, NT], BF16) for _ in range(NPT)]
            PTR = [R(f"PT{i}") for i in range(NPT)]
            rr_ = arena.take([2, NT])
            AA = arena.take([2, NT])
            Lacc = arena.take([2, NT])
            LaccR = [R("Lacc0"), R("Lacc1")]
            Dd = arena.take([NT])
            sq = arena.take([NT], BF16)
            rstd = arena.take([NT])
            epR = R("ep")
            Sps = [PS[0], PS[1]]
            SpsR = [R("S0"), R("S1")]
            Ops, OpsR = PS[2], R("Ops")
            Lps = PS[3]
            Lps0R, Lps1R = R("Lps0"), R("Lps1")
            pend = [None]
            chunks = [blocks[i:i + 16] for i in range(0, len(blocks), 16)]
            crr = [0]
            srr = [0]
            for h in range(H):
                loaded = []
                for ch in chunks:
                    sl = crr[0] % NCH
                    crr[0] += 1
                    ncols = sum(b["nk"] for b in ch)
                    kc0 = ch[0]["kcol"]
                    vb0 = ch[0]["vblk"]
                    rd = [kvR[p] for p in sorted(set(b["pos"] for b in ch))]
                    DMA("sp", KTc[sl][:, 0:ncols], KTs[h, :, kc0:kc0 + ncols], rd, [KTcR[sl]])
                    pr = ch[0]["nk"] if len(ch) == 1 else 128
                    DMA("sp", Vc[sl][0:pr, 0:len(ch), :], Vs[h, 0:pr, vb0:vb0 + len(ch), :], rd, [VcR[sl]])
                    loaded.append(sl)
                flat = []
                for ci, ch in enumerate(chunks):
                    off = 0
                    for bi, b in enumerate(ch):
                        flat.append((loaded[ci], off, bi, b))
                        off += b["nk"]
                nb = len(flat)

                def emit_qk(j):
                    sl, off, bi, b = flat[j]
                    si = srr[0] % 2
                    srr[0] += 1
                    nk, q0 = b["nk"], b["q0"]
                    for m in range(2):
                        MM1(Sps[si][0:nk, m, q0:n], KTc[sl][64 * m:64 * m + 64, off:off + nk],
                            QT[64 * m:64 * m + 64, h, q0:n], True, True, [KTcR[sl], QTR], [SpsR[si]])
                    for (dc, qa, qb, wap) in b["ops"]:
                        sv = Sps[si][0:nk, :, qa:qb]
                        dv = bcast_mid(Dp[0:nk, h, dc:dc + (qb - qa)], 2)
                        if wap is None:
                            VTT(sv, sv, dv, ALU.add, [SpsR[si]], [SpsR[si]])
                        else:
                            VSTT(sv, dv, wselt[0:nk, wap:wap + 1], sv, ALU.mult, ALU.add,
                                 [SpsR[si]], [SpsR[si]])
                    return si

                sidx = {}
                sidx[0] = emit_qk(0)
                if nb > 1:
                    sidx[1] = emit_qk(1)
                if pend[0] is not None:
                    pend[0][0]()
                for j in range(nb):
                    sl, off, bi, b = flat[j]
                    nk, q0 = b["nk"], b["q0"]
                    si = sidx[j]
                    pi = j % NPT
                    bias_ap = cb[0:nk, h:h + 1] if b["bias"] is None else cbm[0:nk, b["bias"], h:h + 1]
                    ACTV(PT[pi][0:nk, :, q0:n], Sps[si][0:nk, :, q0:n], AF.Exp, [SpsR[si]], [PTR[pi]],
                         bias=bias_ap)
                    if j + 2 < nb:
                        sidx[j + 2] = emit_qk(j + 2)
                    for m in range(2):
                        MM1(Ops[:, m, q0:n], Vc[sl][0:nk, bi, :], PT[pi][0:nk, m, q0:n],
                            j == 0, j == nb - 1, [VcR[sl], PTR[pi]], [OpsR])
                    for m in range(2):
                        eng = "dve" if m == 0 else "pool"
                        if m == 1:
                            MM1(Lps[:, 1, q0:n], onesb[0:nk, :], PT[pi][0:nk, 1, q0:n],
                                j == 0, j == nb - 1, [PTR[pi]], [Lps1R])
                        elif j == 0:
                            VCOPY(Lacc[0:nk, m, q0:n], PT[pi][0:nk, m, q0:n], [PTR[pi]], [LaccR[m]], eng=eng)
                        else:
                            VTT(Lacc[0:nk, m, q0:n], Lacc[0:nk, m, q0:n], PT[pi][0:nk, m, q0:n], ALU.add,
                                [PTR[pi], LaccR[m]], [LaccR[m]], eng=eng)
                    if j == 1 and pend[0] is not None:
                        pend[0][1]()
                        pend[0] = None

                def make_ep(h, nb):
                    def ep1():
                        MM1(Lps[:, 0, 0:n], onesf[:, :], Lacc[:, 0, 0:n], True, True, [LaccR[0]], [Lps0R])
                        VRECIP(rr_[:, :, 0:n], Lps[:, :, 0:n], [Lps0R, Lps1R], [epR])
                        VTT(AA[:, :, 0:n], Ops[:, :, 0:n], rr_[:, :, 0:n], ALU.mult, [OpsR, epR], [epR])

                    def ep2():
                        VSTT(Dd[:, 0:n], AA[:, 1, 0:n], nlam[:, 0:1], AA[:, 0, 0:n], ALU.mult, ALU.add,
                             [epR], [epR])
                        VTT(sq[:, 0:n], Dd[:, 0:n], Dd[:, 0:n], ALU.mult, [epR], [epR])
                        MM1(Lps[:, 0, 0:n], onesb[:, :], sq[:, 0:n], True, True, [epR], [Lps0R])
                        ACTV(rstd[:, 0:n], Lps[:, 0, 0:n], AF.Ln, [Lps0R], [epR], bias=epsln[:, :],
                             scale=1.0 / 128)
                        ACTV(rstd[:, 0:n], rstd[:, 0:n], AF.Exp, [epR], [epR], scale=-0.5)
                        VSTT(MT[:, h, 0:n], Dd[:, 0:n], sg08[:, 0:1], rstd[:, 0:n], ALU.mult, ALU.mult,
                             [epR], [MTR])
                    return (ep1, ep2)

                pend[0] = make_ep(h, nb)
            pend[0][0]()
            pend[0][1]()
            pend[0] = None
            prefetch_w(WinS[8], ("in", 8))
            prefetch_w(WinS[9], ("in", 9))
            S.barrier()

        def phaseB3(cfg):
            n, bs, nblk = cfg["n"], cfg["bs"], cfg["nblk"]
            hTc, hTcR = cfg["hT"]
            arena.reset()
            MTf = arena.take([8, NT])
            gtm = [arena.take([NT]) for _ in range(2)]
            MTfR, gtmR = R("MTf"), [R("gt0"), R("gt1")]
            grr = [0]
            for gi in (8, 9, 10, 11):
                wt, wR = load_w(WinS[gi], ("in", gi))
                for ci in range(4):
                    c = ((gi - 8) % 2) * 4 + ci
                    bk, bkR = next_bank()
                    MM(bk[:, 0:n], [(wt[:, kc, ci * 128:(ci + 1) * 128], hTc[:, kc, 0:n]) for kc in range(KC)],
                       [wR, hTcR], [bkR])
                    gs = grr[0] % 2
                    grr[0] += 1
                    col = (gi - 8) * 4 + ci
                    ACTV(gtm[gs][:, 0:n], bk[:, 0:n], AF.Sigmoid, [bkR], [gtmR[gs]], bias=bgate[:, col:col + 1])
                    if gi < 10:
                        VTT(MTf[:, c, 0:n], gtm[gs][:, 0:n], MT[:, c, 0:n], ALU.mult, [gtmR[gs], MTR], [MTfR])
                    else:
                        VTT(gtm[gs][:, 0:n], gtm[gs][:, 0:n], PLT[:, c, 0:n], ALU.mult, [gtmR[gs], PLTR],
                            [gtmR[gs]])
                        VTT(MT[:, c, 0:n], gtm[gs][:, 0:n], MTf[:, c, 0:n], ALU.add, [gtmR[gs], MTfR], [MTR])
            prefetch_w(WoutS[0], ("out", 0))
            prefetch_w(WoutS[1], ("out", 1))
            S.barrier()
            arena.reset()
            x1 = [arena.take([D]) for _ in range(nblk)]
            x1R = x1R_g[:nblk]
            AT = arena.take([NFC, NT], BF16)
            ATR = R("AT")
            yb = [arena.take([D]) for _ in range(nblk)]
            ybR = [R(f"yb{i}") for i in range(nblk)]
            sgt = [arena.take([NT]) for _ in range(2)]
            sgtR = [R("sg0"), R("sg1")]
            w0, w0R = load_w(WoutS[0], ("out", 0))
            w1, w1R = load_w(WoutS[1], ("out", 1))
            h2, h2R = next_hT()
            for blk in range(nblk):
                DMA("sp", x1[blk][0:bs, :], cfg["x"][blk * bs:(blk + 1) * bs, :], (), [x1R[blk]])
            for blk in range(nblk):
                for half, (wt, wR) in enumerate(((w0, w0R), (w1, w1R))):
                    bk, bkR = next_bank()
                    MM(bk[0:bs, :], [(MT[:, kc, blk * bs:(blk + 1) * bs], wt[:, kc, :]) for kc in range(KC)],
                       [wR, MTR], [bkR])
                    VTT(x1[blk][0:bs, half * 512:(half + 1) * 512], bk[0:bs, :],
                        x1[blk][0:bs, half * 512:(half + 1) * 512], ALU.add, [bkR, x1R[blk]], [x1R[blk]])
            frontend_batch([dict(n=bs, gain=gffn, dst=h2[:, :, blk * bs:(blk + 1) * bs], dstR=h2R,
                                 xtile=x1[blk][0:bs, :], xtileR=x1R[blk]) for blk in range(nblk)])
            srr = [0]
            for g in range(11):
                if g == 2 and cfg.get("next") is not None:
                    phaseB1_front(cfg["next"])
                wg, wgR = load_w(WguS[g], ("gu", g), v256)
                wu, wuR = load_w(WguS[11 + g], ("gu", 11 + g), v256)
                for ci in range(2):
                    fc = 2 * g + ci
                    bg, bgR = next_bank()
                    MM(bg[:, 0:n], [(wg[:, kc, ci * 128:(ci + 1) * 128], h2[:, kc, 0:n]) for kc in range(KC)],
                       [wgR, h2R], [bgR])
                    bu, buR = next_bank()
                    MM(bu[:, 0:n], [(wu[:, kc, ci * 128:(ci + 1) * 128], h2[:, kc, 0:n]) for kc in range(KC)],
                       [wuR, h2R], [buR])
                    s_ = srr[0] % 2
                    srr[0] += 1
                    ACTV(sgt[s_][:, 0:n], bg[:, 0:n], AF.Silu, [bgR], [sgtR[s_]])
                    VTT(AT[:, fc, 0:n], sgt[s_][:, 0:n], bu[:, 0:n], ALU.mult, [sgtR[s_], buR], [ATR])
            acc = {}
            k = 0
            for blk in range(nblk):
                for half in range(2):
                    acc[(blk, half)] = bank_k(k)
                    k += 1
            for g in range(11):
                wd, wdR = load_w(WdnS[g], ("dn", g), vdn)
                wdv = vdn(wd)
                for j in range(2):
                    fc = 2 * g + j
                    for blk in range(nblk):
                        for half in range(2):
                            bk, bkR = acc[(blk, half)]
                            MM1(bk[0:bs, :], AT[:, fc, blk * bs:(blk + 1) * bs],
                                wdv[:, j, half * 512:(half + 1) * 512], fc == 0, fc == NFC - 1,
                                [wdR, ATR], [bkR])
            tl = []
            for blk in range(nblk):
                ys = blk
                for half in range(2):
                    bk, bkR = acc[(blk, half)]
                    VTT(yb[ys][0:bs, half * 512:(half + 1) * 512], bk[0:bs, :],
                        x1[blk][0:bs, half * 512:(half + 1) * 512], ALU.add, [bkR, x1R[blk]], [ybR[ys]])
                sl = fe_rr[0] % NFE
                fe_rr[0] += 1
                tl.append((blk, ys, sl, yb[ys][0:bs, :]))
            for (blk, ys, sl, ya) in tl:
                VSTT(hb[sl][0:bs, :], ya, 1.0, ya, ALU.mult, ALU.mult, [ybR[ys]], [hbR[sl], ssR[sl]],
                     accum=ss[sl][0:bs, :])
            for (blk, ys, sl, ya) in tl:
                ACTV(rs[sl][0:bs, :], ss[sl][0:bs, :], AF.Ln, [ssR[sl]], [rsR[sl]],
                     bias=epsmx[0:bs, :], scale=1.0 / D)
            for (blk, ys, sl, ya) in tl:
                ACTV(rs[sl][0:bs, :], rs[sl][0:bs, :], AF.Exp, [rsR[sl]], [rsR[sl]], scale=-0.5)
            for (blk, ys, sl, ya) in tl:
                VSTT(ya, ya, rs[sl][0:bs, 0:1], gfin[0:bs, :], ALU.mult, ALU.mult, [rsR[sl], ybR[ys]], [ybR[ys]])
                DMA(STQ, cfg["y_out"][blk * bs:(blk + 1) * bs, :], ya, [ybR[ys]], [outR], nowaw=True)
            if cfg.get("next") is not None or cfg.get("pf_next"):
                prefetch_w(WinS[6], ("in", 6))
                prefetch_w(WinS[7], ("in", 7))
            bank_rr[0] = 0
            S.barrier()

        def prompt_blocks(i):
            par = i % 2
            blocks = []
            for jb in range(8 * i + 8):
                p, r, bb = jb // 8, (jb % 8) // 4, jb % 4
                b = dict(kcol=jb * 128, vblk=jb, nk=128, pos=jb // 4, q0=0, ops=[], bias=None)
                if p == i - 1 and r == 0 and bb == 3:
                    b["ops"] = [(128, 0, 128, 2 * par)]
                if p == i and r == 0:
                    b["q0"] = 128 * bb
                    b["ops"] = [(0, 128 * bb, min(NT, 128 * bb + 256), None)]
                if p == i and r == 1:
                    b["bias"] = par
                    if bb == 3:
                        b["ops"] = [(128, 0, 128, 2 * par + 1)]
                blocks.append(b)
            return blocks

        nslot_run = NSL if stage >= 1 else 0
        cfgs = [dict(n=NT, bs=128, nblk=4, x=x_own[i], prev=x_prev[i], slot=i, pos=2 * i,
                     kcol=2 * i * 512, vblk=2 * i * 4, k_out=k_own[i], v_out=v_own[i], y_out=y_own[i],
                     tail=(pool_tail, 113) if i == NSLOT - 1 else None) for i in range(nslot_run)]
        for i in range(nslot_run):
            cfg = cfgs[i]
            if stage >= 2 and i + 1 < nslot_run:
                cfg["next"] = cfgs[i + 1]
            phaseB1(cfg)
            if stage >= 2:
                phaseB2(cfg, prompt_blocks(i))
                phaseB3(cfg)

        if stage >= 3:
            arena.reset()
            ckb = arena.take([4, D], BF16)
            ckbR = ckbR_g
            for g4 in range(4):
                DMA("pool", ckb, ck[g4 * 512:(g4 + 1) * 512, :].rearrange("(b p) c -> p b c", p=128), (), [ckbR])
                for blk in range(4):
                    bk, bkR = next_bank()
                    pb = bk.bitcast(BF16)
                    TRS([(pb[:, hh * 128:(hh + 1) * 128], ckb[:, blk, hh * 128:(hh + 1) * 128]) for hh in range(H)],
                        identb[:, :], [ckbR], [bkR])
                    VCOPY(KTst[:, :, blk * 128:(blk + 1) * 128], pb.rearrange("p (k t) -> p k t", k=H),
                          [bkR], [KTstR])
                DMA(STQ, KTs_v[:, :, KT_S + g4 * 512:KT_S + (g4 + 1) * 512], KTst[:], [KTstR], [kvR[16]],
                    nowaw=True)
                DMA("pool", Vst[:], cv[g4 * 512:(g4 + 1) * 512, :].rearrange("(b p) (h e) -> p h b e", p=128, h=H),
                    (), [VstR])
                DMA(STQ, Vs_v[:, :, VB_S + g4 * 4:VB_S + (g4 + 1) * 4, :], Vst[:], [VstR], [kvR[16]],
                    nowaw=True)
            S.barrier()
            cfg = dict(n=NS, bs=NS, nblk=1, x=xs, prev=None, slot=NSLOT, pos=16, kcol=KT_N, vblk=VB_N,
                       k_out=k_s, v_out=v_s, y_out=y_s, tail=(pool_s, 17))
            phaseB1(cfg)
            blocks = []
            for jb in range(16):
                b = dict(kcol=KT_S + jb * 128, vblk=VB_S + jb, nk=128, pos=16, q0=0, ops=[], bias=None)
                if jb == 15:
                    b["ops"] = [(128, 0, NS, None)]
                blocks.append(b)
            blocks.append(dict(kcol=KT_N, vblk=VB_N, nk=NS, pos=16, q0=0, ops=[(0, 0, NS, None)], bias=None))
            phaseB2(cfg, blocks)
            phaseB3(cfg)

        S.add("sp", None, [outR], [])
        S.add("pool", None, [outR], [])

        engsem = {e: es.enter_context(nc.semaphore(f"sem_{e}")) for e in ENGS}
        dsem = {}
        for q in ENGS:
            for i in range(len(S.qs[q])):
                dsem[(q, i)] = es.enter_context(nc.semaphore(f"d_{q}_{i}"))
        cnt = {e: 0 for e in ENGS}
        for op in S.all:
            if not op.dma and op.needed and op.fn is not None:
                cnt[op.eng] += 1
                op.tick = cnt[op.eng]
        blk_ = es.enter_context(nc.Block())
        stats = {e: 0 for e in ENGS}

        def emit(ename, e):
            waited = {}
            for op in S.ops[ename]:
                need = {}
                for d in op.deps:
                    if d.dma:
                        sem, val = dsem[d.key], d.val
                    else:
                        if d.eng == "pe" and ename == "pe":
                            continue
                        if d.fn is None:
                            continue
                        sem, val = engsem[d.eng], d.tick
                    key = sem.num
                    if need.get(key, (None, 0))[1] < val:
                        need[key] = (sem, val)
                for key, (sem, val) in need.items():
                    if waited.get(key, 0) < val:
                        e.wait_ge(sem, val)
                        waited[key] = val
                        stats[ename] += 1
                if op.fn is None:
                    continue
                ins = op.fn(e)
                if op.dma:
                    ins.then_inc(dsem[op.key], 16)
                elif op.needed:
                    ins.then_inc(engsem[ename], 1)

        @blk_.tensor
        def _(e):
            emit("pe", e)

        @blk_.scalar
        def _(e):
            emit("act", e)

        @blk_.vector
        def _(e):
            emit("dve", e)

        @blk_.gpsimd
        def _(e):
            emit("pool", e)

        @blk_.sync
        def _(e):
            emit("sp", e)

        print("ops", {e: len(S.ops[e]) for e in ENGS}, "waits", stats, "sems", len(dsem) + 5)
    return nc


def _own_tile(i, half):
    return 2 * i + (half if i % 2 == 0 else 1 - half)


_CACHE = {}


def kernel(x_prompt, x_sample, cache_k, cache_v, state_pool, rel_bias, norm_mix, w_in, b_gate,
           lambda_q1, lambda_k1, lambda_q2, lambda_k2, subln_g, w_pool, pool_scale, w_out,
           norm_ffn, w_gate_up, w_down, norm_final, _stage=None):
    stage = STAGE if _stage is None else _stage
    f32 = np.float32
    A = lambda a: np.ascontiguousarray(np.asarray(a), dtype=f32)
    x_prompt = A(x_prompt)
    x_sample = A(x_sample)
    cache_k = A(cache_k)
    cache_v = A(cache_v)
    state_pool = A(state_pool)
    B = x_prompt.shape[0]
    xt = x_prompt.reshape(B, 16, NT, D)
    ew = _bucket_consts()
    kk = np.arange(128)[:, None]
    cc = np.arange(256)[None, :]
    maskt = np.where((kk // 64) > (cc // 64), NEG, 0.0).astype(f32)
    jf = np.eye(128, dtype=f32)[::-1].copy()
    common = {
        "w_in": A(w_in)[0], "b_gate": np.ascontiguousarray(A(b_gate)[0].reshape(16, 128).T),
        "w_pool": A(w_pool)[0], "pool_scale": np.ascontiguousarray(A(pool_scale)[0].reshape(8, 128).T),
        "w_out": A(w_out)[0], "w_gu": A(w_gate_up)[0], "w_down": A(w_down)[0],
        "norm_mix": A(norm_mix).reshape(1, D), "norm_ffn": A(norm_ffn).reshape(1, D),
        "norm_final": A(norm_final).reshape(1, D), "subln_g": A(subln_g).reshape(128, 1),
        "lamv": np.concatenate([A(lambda_q1)[0], A(lambda_k1)[0], A(lambda_q2)[0], A(lambda_k2)[0]]).reshape(1, 256),
        "rel_bias": A(rel_bias),
        "ident": np.eye(128, dtype=f32).astype(ml_dtypes.bfloat16), "identf": np.eye(128, dtype=f32),
        "ewin": ew, "jf": jf, "maskt": maskt,
    }
    in_maps = []
    for c in range(8):
        b, half = c // 2, c % 2
        own = [_own_tile(i, half) for i in range(NSLOT)]
        frn = [_own_tile(i, 1 - half) for i in range(NSLOT)]
        x_own = np.ascontiguousarray(xt[b, own])
        x_for = np.ascontiguousarray(xt[b, frn])
        x_prev = np.zeros((NSLOT, 16, D), f32)
        invc = np.zeros((NSLOT + 1, 4, NT), f32)
        for i, t in enumerate(own):
            if t > 0:
                x_prev[i] = x_prompt[b, t * NT - 16:t * NT]
            pos = t * NT + np.arange(NT)
            for g, w in enumerate((2, 4, 8, 16)):
                invc[i, g] = 1.0 / np.minimum(pos + 1, w)
        for g, w in enumerate((2, 4, 8, 16)):
            invc[NSLOT, g] = 1.0 / w
        wsel = np.zeros((128, 4), f32)
        for par in range(2):
            small = (own[par] == 2 * par)
            wsel[:, 2 * par] = 1.0 if small else 0.0
            wsel[:, 2 * par + 1] = 0.0 if small else 1.0
        hist = np.zeros((16, D), f32)
        hist[1:] = state_pool[0, c]
        m = dict(common)
        m.update({"x_own": x_own, "x_for": x_for, "x_prev": x_prev, "wsel": wsel, "invcnt": invc,
                  "xs": x_sample[c], "hist_s": hist, "ck": cache_k[0, c].reshape(PAST, D),
                  "cv": cache_v[0, c].reshape(PAST, D)})
        in_maps.append(m)
    if stage not in _CACHE:
        _CACHE[stage] = build(stage)
    nc = _CACHE[stage]
    res = run_bass_kernel_spmd(nc, in_maps, core_ids=list(range(8)))
    rr = res.results
    y_prompt = np.zeros((B, 16, NT, D), f32)
    k_prompt = np.zeros((B, 16, NT, D), f32)
    v_prompt = np.zeros((B, 16, NT, D), f32)
    pool_prompt = np.zeros((1, B, 15, D), f32)
    y_sample = np.zeros((8, NS, D), f32)
    k_sample = np.zeros((8, NS, D), f32)
    v_sample = np.zeros((8, NS, D), f32)
    pool_sample = np.zeros((1, 8, 15, D), f32)
    for c in range(8):
        b, half = c // 2, c % 2
        own = [_own_tile(i, half) for i in range(NSLOT)]
        r = rr[c]
        y_prompt[b, own] = r["y_own"]
        k_prompt[b, own] = r["k_own"]
        v_prompt[b, own] = r["v_own"]
        if own[-1] == 15:
            pool_prompt[0, b] = r["pool_tail"]
        y_sample[c] = r["y_s"]
        k_sample[c] = r["k_s"]
        v_sample[c] = r["v_s"]
        pool_sample[0, c] = r["pool_s"]
    return (y_prompt.reshape(B, SEQ, D), y_sample,
            k_prompt.reshape(1, B, SEQ, H, 128), v_prompt.reshape(1, B, SEQ, H, 128),
            pool_prompt, k_sample.reshape(1, 8, NS, H, 128), v_sample.reshape(1, 8, NS, H, 128),
            pool_sample)
```

```python
import math
from contextlib import ExitStack
import numpy as np
import ml_dtypes
import concourse.bass as bass
import concourse.mybir as mybir
from concourse.bass_utils import run_bass_kernel_spmd

F32 = mybir.dt.float32
BF16 = mybir.dt.bfloat16
ALU = mybir.AluOpType
AF = mybir.ActivationFunctionType
AP = bass.AP

D = 1024
NT = 512
NSLOT = 8
H = 8
KC = 8
DFF = 2816
NFC = 22
SEQ = 8192
PAST = 2048
NS = 32
NEG = -30000.0
EPS = 1e-6
SUBLN_EPS = 1e-5
LAM_INIT = 0.8 - 0.6 * math.exp(-0.3 * 0)
KT_S = SEQ
KT_N = SEQ + PAST
KTCOLS = SEQ + PAST + 512
VB_S = 64
VB_N = 80
VBLKS = 81
STAGE = 99
STQ = "act"
NFE = 4


class Res:
    __slots__ = ("name", "w", "rs", "sem", "cnt")

    def __init__(self, name):
        self.name = name
        self.w = None
        self.rs = []
        self.sem = {}
        self.cnt = {"hw": 0, "sw": 0}


class Op:
    __slots__ = ("eng", "fn", "deps", "dma", "res", "val", "needed", "tick", "key", "nobar")

    def __init__(self, eng, fn, dma):
        self.eng = eng
        self.fn = fn
        self.dma = dma
        self.deps = []
        self.res = None
        self.val = 0
        self.needed = False
        self.tick = 0
        self.nobar = False


ENGS = ("pe", "act", "dve", "pool", "sp")
QK = {"sp": 16, "act": 8, "pool": 8, "pe": 1, "dve": 1}


class Sched:
    def __init__(self):
        self.ops = {e: [] for e in ENGS}
        self.all = []
        self.bar = []
        self.dma_since = []
        self.dmares = []
        self.qs = {e: [] for e in ENGS}
        self.qn = {e: 0 for e in ENGS}

    def add(self, eng, fn, reads=(), writes=(), dma=False, nowaw=False, extra=(), nobar=False):
        op = Op(eng, fn, dma)
        op.nobar = nobar
        deps = set(self.bar)
        deps.update(extra)
        for r in reads:
            if r.w is not None:
                deps.add(r.w)
        for r in writes:
            if r.w is not None and not nowaw:
                deps.add(r.w)
            for q in r.rs:
                deps.add(q)
        for r in reads:
            r.rs.append(op)
        for r in writes:
            if nowaw:
                r.w = op
            else:
                r.w = op
                r.rs = []
        if dma:
            K = QK[eng]
            lst = self.qs[eng]
            i = self.qn[eng] % K
            self.qn[eng] += 1
            if len(lst) <= i:
                lst.append({"cnt": 0, "last": None})
            ent = lst[i]
            if ent["last"] is not None:
                deps.add(ent["last"])
            ent["cnt"] += 16
            ent["last"] = op
            op.key = (eng, i)
            op.val = ent["cnt"]
            if not nobar:
                self.dma_since.append(op)
        deps.discard(op)
        op.deps = list(deps)
        for d in op.deps:
            d.needed = True
        self.ops[eng].append(op)
        self.all.append(op)
        return op

    def barrier(self):
        b = []
        for e in ENGS:
            for o in reversed(self.ops[e]):
                if not o.nobar:
                    b.append(o)
                    break
        b.extend(self.dma_since)
        self.dma_since = []
        self.bar = b


def _bucket_consts():
    import jax
    import jax.numpy as jnp
    cpu = jax.devices("cpu")[0]
    with jax.default_device(cpu):
        rel = jnp.asarray(127 - np.arange(383), dtype=jnp.int32)
        nb = 16
        ret = jnp.where(rel > 0, nb, 0)
        n = jnp.abs(rel)
        max_exact = nb // 2
        large = max_exact + (jnp.log(jnp.maximum(n, 1).astype(jnp.float32) / max_exact)
                             / math.log(128 / max_exact) * (nb - max_exact)).astype(jnp.int32)
        large = jnp.minimum(large, nb - 1)
        bk = np.asarray(ret + jnp.where(n < max_exact, n, large))
    ew = np.zeros((32, 383), np.float32)
    ew[bk, np.arange(383)] += 1.0
    ew[15, :] -= 1.0
    return ew


def build(stage=STAGE):
    nc = bass.Bass("TRN2", target_bir_lowering=False)

    def din(name, shape, dt=F32):
        return nc.dram_tensor(name, list(shape), dt, kind="ExternalInput").ap()

    def dout(name, shape, dt=F32):
        return nc.dram_tensor(name, list(shape), dt, kind="ExternalOutput").ap()

    def dscr(name, shape, dt):
        return nc.dram_tensor(name, list(shape), dt).ap()

    x_own = din("x_own", [NSLOT, NT, D])
    x_for = din("x_for", [NSLOT, NT, D])
    x_prev = din("x_prev", [NSLOT, 16, D])
    wsel = din("wsel", [128, 4])
    invcnt = din("invcnt", [NSLOT + 1, 4, NT])
    xs = din("xs", [NS, D])
    hist_s = din("hist_s", [16, D])
    ck = din("ck", [PAST, D])
    cv = din("cv", [PAST, D])
    w_in = din("w_in", [D, 6 * D])
    b_gate = din("b_gate", [128, 16])
    w_pool = din("w_pool", [4, 256, 256])
    pool_scale = din("pool_scale", [128, 8])
    w_out = din("w_out", [D, D])
    w_gu = din("w_gu", [D, 2 * DFF])
    w_down = din("w_down", [DFF, D])
    norm_mix = din("norm_mix", [1, D])
    norm_ffn = din("norm_ffn", [1, D])
    norm_final = din("norm_final", [1, D])
    subln_g = din("subln_g", [128, 1])
    lamv = din("lamv", [1, 256])
    rel_bias = din("rel_bias", [32, 8])
    ident_d = din("ident", [128, 128], BF16)
    identf_d = din("identf", [128, 128])
    ewin_d = din("ewin", [32, 383])
    jf_d = din("jf", [128, 128])
    maskt_d = din("maskt", [128, 256])

    y_own = dout("y_own", [NSLOT, NT, D])
    k_own = dout("k_own", [NSLOT, NT, D])
    v_own = dout("v_own", [NSLOT, NT, D])
    pool_tail = dout("pool_tail", [15, D])
    y_s = dout("y_s", [NS, D])
    k_s = dout("k_s", [NS, D])
    v_s = dout("v_s", [NS, D])
    pool_s = dout("pool_s", [15, D])

    WinS = dscr("WinS", [12, 128, KC, 512], BF16)
    WoutS = dscr("WoutS", [2, 128, KC, 512], BF16)
    WguS = dscr("WguS", [22, 128, KC, 256], BF16)
    WdnS = dscr("WdnS", [11, 128, 2, D], BF16)
    KTs = dscr("KTs", [H, 128, KTCOLS], BF16)
    Vs = dscr("Vs", [H, 128, VBLKS, 128], BF16)
    gtab = dscr("gtab", [8, 383], F32)
    KTs_v = KTs.rearrange("h d c -> d h c")
    Vs_v = Vs.rearrange("h p b e -> p h b e")

    S = Sched()
    es = ExitStack()
    with es:
        def sb(name, shape, dt=F32):
            return es.enter_context(nc.sbuf_tensor("sb_" + name, list(shape), dt))

        R = Res

        identb = sb("identb", [128, 128], BF16)
        identf = sb("identf", [128, 128])
        onesb = sb("onesb", [128, 128], BF16)
        onesf = sb("onesf", [128, 128])
        gmix = sb("gmix", [128, D])
        gffn = sb("gffn", [128, D])
        gfin = sb("gfin", [128, D])
        bgate = sb("bgate", [128, 16])
        pscale = sb("pscale", [128, 8])
        sg08 = sb("sg08", [128, 1])
        lamt = sb("lamt", [128, 256])
        lamj = sb("lamj", [128, 64])
        lam2 = sb("lam2", [128, 2])
        nlam = sb("nlam", [128, 1])
        cb = sb("cb", [128, 8])
        cbm = sb("cbm", [128, 2, 8])
        negt = sb("negt", [128, 8])
        wselt = sb("wselt", [128, 4])
        Dp = sb("Dp", [128, 8, 256])
        wpool = sb("wpool", [128, 4, 2, 256], BF16)
        epsln = sb("epsln", [128, 1])
        epsmx = sb("epsmx", [128, 1])
        constR = R("const")
        setupR = R("setup")

        xt = [sb(f"xt{i}", [128, D]) for i in range(NFE)]
        xtR = [R(f"xt{i}") for i in range(NFE)]
        hb = [sb(f"hb{i}", [128, D], BF16) for i in range(NFE)]
        hbR = [R(f"hb{i}") for i in range(NFE)]
        ss = [sb(f"ss{i}", [128, 1]) for i in range(NFE)]
        ssR = [R(f"ss{i}") for i in range(NFE)]
        rs = [sb(f"rs{i}", [128, 1]) for i in range(NFE)]
        rsR = [R(f"rs{i}") for i in range(NFE)]
        hTs = [sb(f"hT{i}", [128, KC, NT], BF16) for i in range(2)]
        hTsR = [R(f"hT{i}") for i in range(2)]
        hTp = sb("hTp", [128, KC, 16], BF16)
        hTpR = R("hTp")
        NW = 3
        wst = [sb(f"wst{i}", [128, KC, 512], BF16) for i in range(NW)]
        wstR = [R(f"wst{i}") for i in range(NW)]
        KTst = sb("KTst", [128, H, NT], BF16)
        KTstR = R("KTst")
        Vst = sb("Vst", [128, H, 4, 128], BF16)
        VstR = R("Vst")
        kvo = [sb(f"kvo{i}", [128, 512]) for i in range(2)]
        kvoR = [R(f"kvo{i}") for i in range(2)]
        QT = sb("QT", [128, H, NT], BF16)
        QTR = R("QT")
        PLT = sb("PLT", [128, 8, NT], BF16)
        PLTR = R("PLT")
        MT = sb("MT", [128, 8, NT], BF16)
        MTR = R("MT")
        ARENA_N = 15360
        arena_t = sb("arena", [128, ARENA_N])

        class Arena:
            def __init__(self):
                self.off = 0

            def reset(self):
                self.off = 0

            def take(self, shape, dt=F32):
                n = 1
                for q in shape:
                    n *= q
                nf = n if dt == F32 else (n + 1) // 2
                nf = (nf + 1) // 2 * 2
                assert self.off + nf <= ARENA_N, (self.off, nf)
                a = arena_t[:, self.off:self.off + nf]
                self.off += nf
                if dt != F32:
                    a = a.bitcast(dt)
                a = a[:, 0:n]
                if len(shape) == 2:
                    a = a.rearrange("p (a b) -> p a b", a=shape[0])
                elif len(shape) == 3:
                    a = a.rearrange("p (a b c) -> p a b c", a=shape[0], b=shape[1])
                return a

        arena = Arena()

        PS = [es.enter_context(nc.psum_tensor(f"ps{i}", [128, 2, 512], F32)) for i in range(4)]
        PSR = [[R(f"ps{i}a"), R(f"ps{i}b")] for i in range(4)]
        bank_rr = [0]

        def bank_k(k):
            return PS[k // 2][:, k % 2, :], PSR[k // 2][k % 2]

        def next_bank():
            k = bank_rr[0] % 8
            bank_rr[0] += 1
            return bank_k(k)

        outR = R("outputs")
        invcR_g, hsR_g, ckbR_g = R("invc"), R("hs"), R("ckb")
        KTcR_g = [R(f"KTc{i}") for i in range(4)]
        VcR_g = [R(f"Vc{i}") for i in range(4)]
        x1R_g = [R(f"x1_{i}") for i in range(4)]
        wscrR = R("wscr")
        kvR = [R(f"kvpos{p}") for p in range(17)]
        gtabR = R("gtab")

        pool_dmas = []

        def DMA(q, out, in_, reads, writes, nowaw=False, nobar=False):
            def fn(e):
                return e.dma_start(out=out, in_=in_)
            extra = ()
            if q == "pool" and len(pool_dmas) >= 3:
                extra = (pool_dmas[-3],)
            op = S.add(q, fn, reads, writes, dma=True, nowaw=nowaw, extra=extra, nobar=nobar)
            if q == "pool":
                pool_dmas.append(op)
            return op

        def MM(out, pairs, reads, writes):
            def fn(e):
                n = len(pairs)
                ins = None
                for i, (l, r) in enumerate(pairs):
                    ins = e.matmul(out, lhsT=l, rhs=r, start=(i == 0), stop=(i == n - 1))
                return ins
            return S.add("pe", fn, reads, writes)

        def MM1(out, l, r, start, stop, reads, writes, skip=False):
            def fn(e):
                if skip:
                    return e.matmul(out, lhsT=l, rhs=r, start=start, stop=stop, skip_group_check=True)
                return e.matmul(out, lhsT=l, rhs=r, start=start, stop=stop)
            return S.add("pe", fn, reads, writes)

        def TRS(items, ident, reads, writes):
            def fn(e):
                ins = None
                for (o, i) in items:
                    ins = e.transpose(o, i, ident)
                return ins
            return S.add("pe", fn, reads, writes)

        def ACTV(out, in_, func, reads, writes, bias=None, scale=None):
            def fn(e):
                kw = {}
                if bias is not None:
                    kw["bias"] = bias
                if scale is not None:
                    kw["scale"] = scale
                return e.activation(out=out, in_=in_, func=func, **kw)
            return S.add("act", fn, reads, writes)

        def VCOPY(out, in_, reads, writes, eng="dve"):
            def fn(e):
                return e.tensor_copy(out=out, in_=in_)
            return S.add(eng, fn, reads, writes)

        def VTT(out, in0, in1, op, reads, writes, eng="dve"):
            def fn(e):
                return e.tensor_tensor(out=out, in0=in0, in1=in1, op=op)
            return S.add(eng, fn, reads, writes)

        def VTS(out, in0, s1, op0, reads, writes, eng="dve"):
            def fn(e):
                return e.tensor_scalar(out=out, in0=in0, scalar1=s1, scalar2=None, op0=op0)
            return S.add(eng, fn, reads, writes)

        def VSTT(out, in0, scalar, in1, op0, op1, reads, writes, accum=None):
            def fn(e):
                if accum is not None:
                    return e.scalar_tensor_tensor(out=out, in0=in0, scalar=scalar, in1=in1,
                                                  op0=op0, op1=op1, accum_out=accum)
                return e.scalar_tensor_tensor(out=out, in0=in0, scalar=scalar, in1=in1,
                                              op0=op0, op1=op1)
            return S.add("dve", fn, reads, writes)

        def VRECIP(out, in_, reads, writes):
            def fn(e):
                return e.reciprocal(out=out, in_=in_)
            return S.add("dve", fn, reads, writes)

        def MEMSET(t, val, writes, eng="pool"):
            def fn(e):
                return e.memset(t, val)
            return S.add(eng, fn, (), writes)

        def bcast_rows(ap2d, nparts):
            n = ap2d.shape[-1]
            return AP(ap2d.tensor, ap2d.offset, [[0, nparts], [1, n]])

        def bcast_mid(a, m):
            apl = a.ap
            return AP(a.tensor, a.offset, [list(apl[0]), [0, m], list(apl[-1])])

        for (t, src) in ((identb[:], ident_d), (identf[:], identf_d),
                         (gmix[:], bcast_rows(norm_mix, 128)), (gffn[:], bcast_rows(norm_ffn, 128)),
                         (gfin[:], bcast_rows(norm_final, 128)), (bgate[:], b_gate),
                         (pscale[:], pool_scale), (sg08[:], subln_g),
                         (lamt[:], bcast_rows(lamv, 128)), (wselt[:], wsel),
                         (cb[:], bcast_rows(rel_bias[15:16, :], 128))):
            DMA("sp", t, src, (), [constR], nowaw=True)
        DMA("pool", wpool[:], w_pool.rearrange("g (cc p) d -> p g cc d", p=128), (), [constR], nowaw=True)
        MEMSET(onesb[:], 1.0, [setupR])
        MEMSET(onesf[:], 1.0, [setupR])
        MEMSET(epsln[:], SUBLN_EPS, [setupR])
        MEMSET(epsmx[:], EPS, [setupR])
        MEMSET(negt[:], NEG, [setupR])
        if True:
            arena.reset()
            ewin = arena.take([384])[0:32, 0:383]
            rbt = arena.take([8])[0:32, :]
            jt = arena.take([128])
            maskt = arena.take([256])
            gsb = arena.take([384])[0:8, 0:383]
            Hk = arena.take([8, 256])
            c2R = R("const2")
            for (t, src) in ((ewin, ewin_d), (rbt, rel_bias), (jt, jf_d), (maskt, maskt_d)):
                DMA("sp", t, src, (), [c2R], nowaw=True)
            S.barrier()
            VTS(sg08[:], sg08[:], 1.0 - LAM_INIT, ALU.mult, [setupR], [setupR])
            VSTT(lamj[:], lamt[:, 0:64], 1.0, lamt[:, 64:128], ALU.mult, ALU.mult, [setupR], [setupR],
                 accum=lam2[:, 0:1])
            VSTT(lamj[:], lamt[:, 128:192], 1.0, lamt[:, 192:256], ALU.mult, ALU.mult, [setupR], [setupR],
                 accum=lam2[:, 1:2])
            ACTV(lam2[:], lam2[:], AF.Exp, [setupR], [setupR])
            VTT(nlam[:], lam2[:, 1:2], lam2[:, 0:1], ALU.subtract, [setupR], [setupR])
            VTS(nlam[:], nlam[:], -LAM_INIT, ALU.add, [setupR], [setupR])
            for par in range(2):
                VSTT(cbm[:, par, :], negt[:], wselt[:, 2 * par:2 * par + 1], cb[:], ALU.mult, ALU.add,
                     [setupR], [setupR])
            bk, bkR = next_bank()
            MM(bk[0:8, 0:383], [(rbt, ewin)], [setupR], [bkR])
            VCOPY(gsb, bk[0:8, 0:383], [bkR], [setupR])
            DMA("sp", gtab, gsb, [setupR], [gtabR])
            hkR = R("Hk")
            DMA("sp", Hk, AP(gtab.tensor, gtab.offset, [[1, 128], [383, 8], [1, 256]]), [gtabR], [hkR])
            for hp in range(4):
                bk, bkR = next_bank()
                for j in range(2):
                    MM(bk[:, j * 256:(j + 1) * 256], [(jt, Hk[:, 2 * hp + j, :])], [hkR, setupR], [bkR])
                VTT(Dp[:, 2 * hp:2 * hp + 2, :], bk.rearrange("p (a b) -> p a b", a=2),
                    bcast_mid(maskt, 2), ALU.add, [bkR, setupR], [setupR])
            S.barrier()

        wrr = [0]

        arena.reset()
        cst = [arena.take([KC, 512], BF16) for _ in range(2)]
        cstR = [R("cst0"), R("cst1")]
        crr_ = [0]
        scrR = {}

        def conv(src, dst, key, view=None):
            sl = crr_[0] % 2
            crr_[0] += 1
            tl = cst[sl] if view is None else view(cst[sl])
            scrR[key] = [R("scr")]
            DMA("pool", tl, src, (), [cstR[sl]])
            DMA(STQ, dst, tl, [cstR[sl]], [scrR[key][0]])

        w_in_v = w_in.rearrange("(kc p) c -> p kc c", p=128)
        w_out_v = w_out.rearrange("(kc p) c -> p kc c", p=128)
        w_gu_v = w_gu.rearrange("(kc p) c -> p kc c", p=128)
        w_dn_v = w_down.rearrange("(fc p) c -> p fc c", p=128)

        def v256(t):
            return t[:, :, 0:256]

        def vdn(t):
            return t[:, :, :].rearrange("p a b -> p (a b)")[:, 0:2048].rearrange("p (a b) -> p a b", a=2)

        wkv = [arena.take([KC, 512], BF16) for _ in range(4)]
        wkvR = [R(f"wkv{i}") for i in range(4)]
        if stage >= 0:
            for idx, g in enumerate((2, 3, 4, 5)):
                DMA("pool", wkv[idx], w_in_v[:, :, g * 512:(g + 1) * 512], (), [wkvR[idx]])
                scrR[("in", g)] = [R("scr")]
                DMA(STQ, WinS[g], wkv[idx], [wkvR[idx]], [scrR[("in", g)][0]])
        conv_list = []
        for g in (6, 7, 0, 1):
            conv_list.append((w_in_v[:, :, g * 512:(g + 1) * 512], WinS[g], ("in", g), None))
        conv_pos = [0]
        bst = [sb(f"bst{i}", [128, KC, 256], BF16) for i in range(2)]
        bstR = [R("bst0"), R("bst1")]
        bg_list = []
        for g in (8, 9, 10, 11):
            for hf in range(2):
                bg_list.append((w_in_v[:, :, g * 512 + hf * 256:g * 512 + (hf + 1) * 256],
                                WinS[g][:, :, hf * 256:(hf + 1) * 256], ("in", g), None))
        for g in range(2):
            for hf in range(2):
                bg_list.append((w_out_v[:, :, g * 512 + hf * 256:g * 512 + (hf + 1) * 256],
                                WoutS[g][:, :, hf * 256:(hf + 1) * 256], ("out", g), None))
        for g in range(11):
            bg_list.append((w_gu_v[:, :, g * 256:(g + 1) * 256], WguS[g], ("gu", g), None))
            bg_list.append((w_gu_v[:, :, (11 + g) * 256:(12 + g) * 256], WguS[11 + g], ("gu", 11 + g), None))
        for g in range(11):
            bg_list.append((w_dn_v[:, 2 * g:2 * g + 2, :], WdnS[g], ("dn", g), "dn"))

        def conv_bg():
            for i, (src, dst, key, view) in enumerate(bg_list):
                sl = i % 2
                tl = bst[sl][:]
                if view == "dn":
                    tl = bst[sl][:, :, :].rearrange("p a b -> p (a b)").rearrange("p (a b) -> p a b", a=2)
                r = R("scr")
                scrR.setdefault(key, []).append(r)
                DMA("pool", tl, src, (), [bstR[sl]], nobar=True)
                DMA("pool", dst, tl, [bstR[sl]], [r], nobar=True)
            del bg_list[:]

        def conv_some(k):
            for _ in range(k):
                if conv_pos[0] < len(conv_list):
                    a, b, c, d = conv_list[conv_pos[0]]
                    conv_pos[0] += 1
                    conv(a, b, c, d)


        wrr = [0]

        pfw = {}

        def load_w(src, key, view=None):
            if key in pfw:
                return pfw.pop(key)
            sl = wrr[0] % NW
            wrr[0] += 1
            tl = wst[sl][:] if view is None else view(wst[sl])
            DMA("sp", tl, src, scrR[key], [wstR[sl]])
            return wst[sl], wstR[sl]

        def prefetch_w(src, key, view=None):
            pfw[key] = load_w(src, key, view)

        fe_rr = [0]

        def frontend_batch(items):
            st = []
            for it in items:
                sl = fe_rr[0] % NFE
                fe_rr[0] += 1
                n = it["n"]
                if it.get("xsrc") is not None:
                    DMA("sp", xt[sl][0:n, :], it["xsrc"], (), [xtR[sl]])
                    xa, xR = xt[sl][0:n, :], xtR[sl]
                else:
                    xa, xR = it["xtile"], it["xtileR"]
                st.append((sl, n, xa, xR, it))
            for (sl, n, xa, xR, it) in st:
                VSTT(hb[sl][0:n, :], xa, 1.0, xa, ALU.mult, ALU.mult, [xR], [hbR[sl], ssR[sl]],
                     accum=ss[sl][0:n, :])
            for (sl, n, xa, xR, it) in st:
                ACTV(rs[sl][0:n, :], ss[sl][0:n, :], AF.Ln, [ssR[sl]], [rsR[sl]],
                     bias=epsmx[0:n, :], scale=1.0 / D)
            for (sl, n, xa, xR, it) in st:
                ACTV(rs[sl][0:n, :], rs[sl][0:n, :], AF.Exp, [rsR[sl]], [rsR[sl]], scale=-0.5)
            for (sl, n, xa, xR, it) in st:
                VSTT(hb[sl][0:n, :], xa, rs[sl][0:n, 0:1], it["gain"][0:n, :], ALU.mult, ALU.mult,
                     [xR, rsR[sl], hbR[sl]], [hbR[sl]])
            bks = []
            for (sl, n, xa, xR, it) in st:
                bk, bkR = next_bank()
                pb = bk.bitcast(BF16)
                TRS([(pb[:, kc * 128:kc * 128 + n], hb[sl][0:n, kc * 128:(kc + 1) * 128]) for kc in range(KC)],
                    identb[0:n, 0:n], [hbR[sl]], [bkR])
                bks.append((pb, bkR))
            for (sl, n, xa, xR, it), (pb, bkR) in zip(st, bks):
                VCOPY(it["dst"], pb.rearrange("p (k t) -> p k t", k=KC)[:, :, 0:n], [bkR], [it["dstR"]])

        def frontend(n, gain, dst, dstR, xsrc=None, xtile=None, xtileR=None):
            frontend_batch([dict(n=n, gain=gain, dst=dst, dstR=dstR, xsrc=xsrc, xtile=xtile, xtileR=xtileR)])

        kvo_rr = [0]

        def kvo_slot():
            sl = kvo_rr[0] % 2
            kvo_rr[0] += 1
            return kvo[sl], kvoR[sl]

        hT_rr = [0]

        def next_hT():
            sl = hT_rr[0] % 2
            hT_rr[0] += 1
            return hTs[sl], hTsR[sl]

        def kt_fm(hTc, hTcR, n, wt, wR, hbase):
            for ci in range(4):
                bk, bkR = next_bank()
                MM(bk[:, 0:n], [(wt[:, kc, ci * 128:(ci + 1) * 128], hTc[:, kc, 0:n]) for kc in range(KC)],
                   [wR, hTcR], [bkR])
                ACTV(KTst[:, hbase + ci, 0:n], bk[:, 0:n], AF.Copy, [bkR], [KTstR])

        def phaseA_front(f):
            hTc, hTcR = next_hT()
            frontend_batch([dict(n=128, gain=gmix, dst=hTc[:, :, blk * 128:(blk + 1) * 128], dstR=hTcR,
                                 xsrc=x_for[f, blk * 128:(blk + 1) * 128, :]) for blk in range(4)])
            return hTc, hTcR

        def phaseA_mm(f, hTc, hTcR):
            pos = 2 * f + 1
            for gi in (2, 3):
                wt, wR = wkv[gi - 2], wkvR[gi - 2]
                kt_fm(hTc, hTcR, NT, wt, wR, (gi - 2) * 4)
            DMA(STQ, KTs_v[:, :, pos * 512:(pos + 1) * 512], KTst[:], [KTstR], [kvR[pos]], nowaw=True)
            for gi in (4, 5):
                wt, wR = wkv[gi - 2], wkvR[gi - 2]
                for blk in range(4):
                    bk, bkR = next_bank()
                    MM(bk[:, :], [(hTc[:, kc, blk * 128:(blk + 1) * 128], wt[:, kc, :]) for kc in range(KC)],
                       [wR, hTcR], [bkR])
                    VCOPY(Vst[:, (gi - 4) * 4:(gi - 4) * 4 + 4, blk, :],
                          bk.rearrange("p (h e) -> p h e", h=4), [bkR], [VstR])
            DMA(STQ, Vs_v[:, :, pos * 4:(pos + 1) * 4, :], Vst[:], [VstR], [kvR[pos]], nowaw=True)

        NSL = NSLOT
        if stage >= 1:
            cur = phaseA_front(0)
            for f in range(NSL):
                nxt = phaseA_front(f + 1) if f + 1 < NSL else None
                conv_some(6)
                phaseA_mm(f, cur[0], cur[1])
                cur = nxt
        conv_some(len(conv_list))
        S.barrier()

        def phaseB1_front(cfg):
            n, bs, nblk = cfg["n"], cfg["bs"], cfg["nblk"]
            hTc, hTcR = next_hT()
            cfg["hT"] = (hTc, hTcR)
            items = [dict(n=bs, gain=gmix, dst=hTc[:, :, blk * bs:(blk + 1) * bs], dstR=hTcR,
                          xsrc=cfg["x"][blk * bs:(blk + 1) * bs, :]) for blk in range(nblk)]
            frontend_batch(items)
            if cfg["prev"] is not None:
                frontend(16, gmix, hTp[:, :, :], hTpR, xsrc=cfg["prev"])

        def phaseB1(cfg):
            n, bs, nblk = cfg["n"], cfg["bs"], cfg["nblk"]
            conv_bg()
            arena.reset()
            UT = arena.take([8, 16 + NT])
            invc = arena.take([4, NT])
            a1 = arena.take([2, 16 + NT])
            a2 = arena.take([2, 16 + NT])
            a3 = arena.take([2, 16 + NT])
            DT = arena.take([8, NT], BF16)
            UTR, invcR, aR, DTR = R("UT"), invcR_g, R("apool"), R("DT")
            if "hT" not in cfg:
                phaseB1_front(cfg)
            hTc, hTcR = cfg["hT"]
            DMA("sp", invc, AP(invcnt.tensor, invcnt[cfg["slot"]].offset, [[0, 128], [NT, 4], [1, NT]]),
                (), [invcR])
            if cfg["prev"] is None:
                hs = arena.take([D])
                hsR = hsR_g
                DMA("sp", hs[0:16, :], hist_s, (), [hsR])
                bk, bkR = next_bank()
                TRS([(bk[:, kc * 16:(kc + 1) * 16], hs[0:16, kc * 128:(kc + 1) * 128]) for kc in range(KC)],
                    identf[0:16, 0:16], [hsR], [bkR])
                VCOPY(UT[:, :, 0:16], bk[:, 0:128].rearrange("p (k t) -> p k t", k=KC), [bkR], [UTR])
            for gi in (6, 7):
                wt, wR = load_w(WinS[gi], ("in", gi))
                for ci in range(4):
                    c = (gi - 6) * 4 + ci
                    bk, bkR = next_bank()
                    MM(bk[:, 0:n], [(wt[:, kc, ci * 128:(ci + 1) * 128], hTc[:, kc, 0:n]) for kc in range(KC)],
                       [wR, hTcR], [bkR])
                    VCOPY(UT[:, c, 16:16 + n], bk[:, 0:n], [bkR], [UTR])
                    if cfg["prev"] is not None:
                        bk, bkR = next_bank()
                        MM(bk[:, 0:16], [(wt[:, kc, ci * 128:(ci + 1) * 128], hTp[:, kc, :]) for kc in range(KC)],
                           [wR, hTpR], [bkR])
                        VCOPY(UT[:, c, 0:16], bk[:, 0:16], [bkR], [UTR])
                if cfg["tail"] is not None:
                    tdst, row0 = cfg["tail"]
                    blk = nblk - 1
                    bk, bkR = next_bank()
                    MM(bk[0:bs, :], [(hTc[:, kc, blk * bs:(blk + 1) * bs], wt[:, kc, :]) for kc in range(KC)],
                       [wR, hTcR], [bkR])
                    ko, koR = kvo_slot()
                    VCOPY(ko[0:bs, :], bk[0:bs, :], [bkR], [koR])
                    DMA(STQ, tdst[:, (gi - 6) * 512:(gi - 5) * 512], ko[row0:row0 + 15, :], [koR], [outR],
                        nowaw=True)
            L = 16 + n
            for g in range(4):
                c0 = 2 * g
                U2 = UT[:, c0:c0 + 2, :]
                VTT(a1[:, :, 1:L], U2[:, :, 1:L], U2[:, :, 0:L - 1], ALU.add, [UTR], [aR])
                cur = a1
                if g >= 1:
                    VTT(a2[:, :, 3:L], a1[:, :, 3:L], a1[:, :, 1:L - 2], ALU.add, [aR], [aR])
                    cur = a2
                if g >= 2:
                    VTT(a3[:, :, 7:L], a2[:, :, 7:L], a2[:, :, 3:L - 4], ALU.add, [aR], [aR])
                    cur = a3
                if g >= 3:
                    VTT(a1[:, :, 15:L], a3[:, :, 15:L], a3[:, :, 7:L - 8], ALU.add, [aR], [aR])
                    cur = a1
                oth = a2 if cur is not a2 else a3
                VTT(oth[:, :, 16:L], cur[:, :, 16:L], bcast_mid(invc[:, g, 0:n], 2), ALU.mult,
                    [aR, invcR], [aR])
                VTT(DT[:, c0:c0 + 2, 0:n], oth[:, :, 16:L], U2[:, :, 16:L], ALU.subtract, [aR, UTR], [DTR])
            for gi in (0, 1):
                wt, wR = load_w(WinS[gi], ("in", gi))
                for ci in range(4):
                    bk, bkR = next_bank()
                    MM(bk[:, 0:n], [(wt[:, kc, ci * 128:(ci + 1) * 128], hTc[:, kc, 0:n]) for kc in range(KC)],
                       [wR, hTcR], [bkR])
                    ACTV(QT[:, gi * 4 + ci, 0:n], bk[:, 0:n], AF.Copy, [bkR], [QTR], scale=0.125)
            for gi in (2, 3):
                wt, wR = load_w(WinS[gi], ("in", gi))
                kt_fm(hTc, hTcR, n, wt, wR, (gi - 2) * 4)
                for blk in range(nblk):
                    bk, bkR = next_bank()
                    MM(bk[0:bs, :], [(hTc[:, kc, blk * bs:(blk + 1) * bs], wt[:, kc, :]) for kc in range(KC)],
                       [wR, hTcR], [bkR])
                    ko, koR = kvo_slot()
                    VCOPY(ko[0:bs, :], bk[0:bs, :], [bkR], [koR])
                    DMA(STQ, cfg["k_out"][blk * bs:(blk + 1) * bs, (gi - 2) * 512:(gi - 1) * 512],
                        ko[0:bs, :], [koR], [outR], nowaw=True)
            DMA(STQ, KTs_v[:, :, cfg["kcol"]:cfg["kcol"] + n], KTst[:, :, 0:n], [KTstR],
                [kvR[cfg["pos"]]], nowaw=True)
            for gi in (4, 5):
                wt, wR = load_w(WinS[gi], ("in", gi))
                for blk in range(nblk):
                    bk, bkR = next_bank()
                    MM(bk[0:bs, :], [(hTc[:, kc, blk * bs:(blk + 1) * bs], wt[:, kc, :]) for kc in range(KC)],
                       [wR, hTcR], [bkR])
                    ko, koR = kvo_slot()
                    VCOPY(ko[0:bs, :], bk[0:bs, :], [bkR], [koR])
                    VCOPY(Vst[0:bs, (gi - 4) * 4:(gi - 4) * 4 + 4, blk, :],
                          ko[0:bs, :].rearrange("p (h e) -> p h e", h=4), [koR], [VstR], eng="pool")
                    DMA(STQ, cfg["v_out"][blk * bs:(blk + 1) * bs, (gi - 4) * 512:(gi - 3) * 512],
                        ko[0:bs, :], [koR], [outR], nowaw=True)
            DMA(STQ, Vs_v[0:bs, :, cfg["vblk"]:cfg["vblk"] + nblk, :], Vst[0:bs, :, 0:nblk, :], [VstR],
                [kvR[cfg["pos"]]], nowaw=True)
            for g in range(4):
                for oc in range(2):
                    bk, bkR = next_bank()
                    MM(bk[:, 0:n], [(wpool[:, g, cc, oc * 128:(oc + 1) * 128], DT[:, 2 * g + cc, 0:n])
                                    for cc in range(2)], [DTR], [bkR])
                    c = 2 * g + oc
                    VTS(PLT[:, c, 0:n], bk[:, 0:n], pscale[:, c:c + 1], ALU.mult, [bkR], [PLTR])
            S.barrier()

        def phaseB2(cfg, blocks):
            n = cfg["n"]
            arena.reset()
            NCH = 4
            KTc = [arena.take([2048], BF16) for _ in range(NCH)]
            Vc = [arena.take([16, 128], BF16) for _ in range(NCH)]
            KTcR = KTcR_g
            VcR = VcR_g
            NPT = 4
            PT = [arena.take([2, NT], BF16) for _ in range(NPT)]
            PTR = [R(f"PT{i}") for i in range(NPT)]
            rr_ = arena.take([2, NT])
            AA = arena.take([2, NT])
            Lacc = arena.take([2, NT])
            LaccR = [R("Lacc0"), R("Lacc1")]
            Dd = arena.take([NT])
            sq = arena.take([NT], BF16)
            rstd = arena.take([NT])
            epR = R("ep")
            Sps = [PS[0], PS[1]]
            SpsR = [R("S0"), R("S1")]
            Ops, OpsR = PS[2], R("Ops")
            Lps = PS[3]
            Lps0R, Lps1R = R("Lps0"), R("Lps1")
            pend = [None]
            chunks = [blocks[i:i + 16] for i in range(0, len(blocks), 16)]
            crr = [0]
            srr = [0]
            for h in range(H):
                loaded = []
                for ch in chunks:
                    sl = crr[0] % NCH
                    crr[0] += 1
                    ncols = sum(b["nk"] for b in ch)
                    kc0 = ch[0]["kcol"]
                    vb0 = ch[0]["vblk"]
                    rd = [kvR[p] for p in sorted(set(b["pos"] for b in ch))]
                    DMA("sp", KTc[sl][:, 0:ncols], KTs[h, :, kc0:kc0 + ncols], rd, [KTcR[sl]])
                    pr = ch[0]["nk"] if len(ch) == 1 else 128
                    DMA("sp", Vc[sl][0:pr, 0:len(ch), :], Vs[h, 0:pr, vb0:vb0 + len(ch), :], rd, [VcR[sl]])
                    loaded.append(sl)
                flat = []
                for ci, ch in enumerate(chunks):
                    off = 0
                    for bi, b in enumerate(ch):
                        flat.append((loaded[ci], off, bi, b))
                        off += b["nk"]
                nb = len(flat)

                def emit_qk(j):
                    sl, off, bi, b = flat[j]
                    si = srr[0] % 2
                    srr[0] += 1
                    nk, q0 = b["nk"], b["q0"]
                    for m in range(2):
                        MM1(Sps[si][0:nk, m, q0:n], KTc[sl][64 * m:64 * m + 64, off:off + nk],
                            QT[64 * m:64 * m + 64, h, q0:n], True, True, [KTcR[sl], QTR], [SpsR[si]])
                    for (dc, qa, qb, wap) in b["ops"]:
                        sv = Sps[si][0:nk, :, qa:qb]
                        dv = bcast_mid(Dp[0:nk, h, dc:dc + (qb - qa)], 2)
                        if wap is None:
                            VTT(sv, sv, dv, ALU.add, [SpsR[si]], [SpsR[si]])
                        else:
                            VSTT(sv, dv, wselt[0:nk, wap:wap + 1], sv, ALU.mult, ALU.add,
                                 [SpsR[si]], [SpsR[si]])
                    return si

                sidx = {}
                sidx[0] = emit_qk(0)
                if nb > 1:
                    sidx[1] = emit_qk(1)
                if pend[0] is not None:
                    pend[0][0]()
                for j in range(nb):
                    sl, off, bi, b = flat[j]
                    nk, q0 = b["nk"], b["q0"]
                    si = sidx[j]
                    pi = j % NPT
                    bias_ap = cb[0:nk, h:h + 1] if b["bias"] is None else cbm[0:nk, b["bias"], h:h + 1]
                    ACTV(PT[pi][0:nk, :, q0:n], Sps[si][0:nk, :, q0:n], AF.Exp, [SpsR[si]], [PTR[pi]],
                         bias=bias_ap)
                    if j + 2 < nb:
                        sidx[j + 2] = emit_qk(j + 2)
                    for m in range(2):
                        MM1(Ops[:, m, q0:n], Vc[sl][0:nk, bi, :], PT[pi][0:nk, m, q0:n],
                            j == 0, j == nb - 1, [VcR[sl], PTR[pi]], [OpsR])
                    for m in range(2):
                        eng = "dve" if m == 0 else "pool"
                        if m == 1:
                            MM1(Lps[:, 1, q0:n], onesb[0:nk, :], PT[pi][0:nk, 1, q0:n],
                                j == 0, j == nb - 1, [PTR[pi]], [Lps1R])
                        elif j == 0:
                            VCOPY(Lacc[0:nk, m, q0:n], PT[pi][0:nk, m, q0:n], [PTR[pi]], [LaccR[m]], eng=eng)
                        else:
                            VTT(Lacc[0:nk, m, q0:n], Lacc[0:nk, m, q0:n], PT[pi][0:nk, m, q0:n], ALU.add,
                                [PTR[pi], LaccR[m]], [LaccR[m]], eng=eng)
                    if j == 1 and pend[0] is not None:
                        pend[0][1]()
                        pend[0] = None

                def make_ep(h, nb):
                    def ep1():
                        MM1(Lps[:, 0, 0:n], onesf[:, :], Lacc[:, 0, 0:n], True, True, [LaccR[0]], [Lps0R])
                        VRECIP(rr_[:, :, 0:n], Lps[:, :, 0:n], [Lps0R, Lps1R], [epR])
                        VTT(AA[:, :, 0:n], Ops[:, :, 0:n], rr_[:, :, 0:n], ALU.mult, [OpsR, epR], [epR])

                    def ep2():
                        VSTT(Dd[:, 0:n], AA[:, 1, 0:n], nlam[:, 0:1], AA[:, 0, 0:n], ALU.mult, ALU.add,
                             [epR], [epR])
                        VTT(sq[:, 0:n], Dd[:, 0:n], Dd[:, 0:n], ALU.mult, [epR], [epR])
                        MM1(Lps[:, 0, 0:n], onesb[:, :], sq[:, 0:n], True, True, [epR], [Lps0R])
                        ACTV(rstd[:, 0:n], Lps[:, 0, 0:n], AF.Ln, [Lps0R], [epR], bias=epsln[:, :],
                             scale=1.0 / 128)
                        ACTV(rstd[:, 0:n], rstd[:, 0:n], AF.Exp, [epR], [epR], scale=-0.5)
                        VSTT(MT[:, h, 0:n], Dd[:, 0:n], sg08[:, 0:1], rstd[:, 0:n], ALU.mult, ALU.mult,
                             [epR], [MTR])
                    return (ep1, ep2)

                pend[0] = make_ep(h, nb)
            pend[0][0]()
            pend[0][1]()
            pend[0] = None
            prefetch_w(WinS[8], ("in", 8))
            prefetch_w(WinS[9], ("in", 9))
            S.barrier()

        def phaseB3(cfg):
            n, bs, nblk = cfg["n"], cfg["bs"], cfg["nblk"]
            hTc, hTcR = cfg["hT"]
            arena.reset()
            MTf = arena.take([8, NT])
            gtm = [arena.take([NT]) for _ in range(2)]
            MTfR, gtmR = R("MTf"), [R("gt0"), R("gt1")]
            grr = [0]
            for gi in (8, 9, 10, 11):
                wt, wR = load_w(WinS[gi], ("in", gi))
                for ci in range(4):
                    c = ((gi - 8) % 2) * 4 + ci
                    bk, bkR = next_bank()
                    MM(bk[:, 0:n], [(wt[:, kc, ci * 128:(ci + 1) * 128], hTc[:, kc, 0:n]) for kc in range(KC)],
                       [wR, hTcR], [bkR])
                    gs = grr[0] % 2
                    grr[0] += 1
                    col = (gi - 8) * 4 + ci
                    ACTV(gtm[gs][:, 0:n], bk[:, 0:n], AF.Sigmoid, [bkR], [gtmR[gs]], bias=bgate[:, col:col + 1])
                    if gi < 10:
                        VTT(MTf[:, c, 0:n], gtm[gs][:, 0:n], MT[:, c, 0:n], ALU.mult, [gtmR[gs], MTR], [MTfR])
                    else:
                        VTT(gtm[gs][:, 0:n], gtm[gs][:, 0:n], PLT[:, c, 0:n], ALU.mult, [gtmR[gs], PLTR],
                            [gtmR[gs]])
                        VTT(MT[:, c, 0:n], gtm[gs][:, 0:n], MTf[:, c, 0:n], ALU.add, [gtmR[gs], MTfR], [MTR])
            prefetch_w(WoutS[0], ("out", 0))
            prefetch_w(WoutS[1], ("out", 1))
            S.barrier()
            arena.reset()
            x1 = [arena.take([D]) for _ in range(nblk)]
            x1R = x1R_g[:nblk]
            AT = arena.take([NFC, NT], BF16)
            ATR = R("AT")
            yb = [arena.take([D]) for _ in range(nblk)]
            ybR = [R(f"yb{i}") for i in range(nblk)]
            sgt = [arena.take([NT]) for _ in range(2)]
            sgtR = [R("sg0"), R("sg1")]
            w0, w0R = load_w(WoutS[0], ("out", 0))
            w1, w1R = load_w(WoutS[1], ("out", 1))
            h2, h2R = next_hT()
            for blk in range(nblk):
                DMA("sp", x1[blk][0:bs, :], cfg["x"][blk * bs:(blk + 1) * bs, :], (), [x1R[blk]])
            for blk in range(nblk):
                for half, (wt, wR) in enumerate(((w0, w0R), (w1, w1R))):
                    bk, bkR = next_bank()
                    MM(bk[0:bs, :], [(MT[:, kc, blk * bs:(blk + 1) * bs], wt[:, kc, :]) for kc in range(KC)],
                       [wR, MTR], [bkR])
                    VTT(x1[blk][0:bs, half * 512:(half + 1) * 512], bk[0:bs, :],
                        x1[blk][0:bs, half * 512:(half + 1) * 512], ALU.add, [bkR, x1R[blk]], [x1R[blk]])
            frontend_batch([dict(n=bs, gain=gffn, dst=h2[:, :, blk * bs:(blk + 1) * bs], dstR=h2R,
                                 xtile=x1[blk][0:bs, :], xtileR=x1R[blk]) for blk in range(nblk)])
            srr = [0]
            for g in range(11):
                if g == 2 and cfg.get("next") is not None:
                    phaseB1_front(cfg["next"])
                wg, wgR = load_w(WguS[g], ("gu", g), v256)
                wu, wuR = load_w(WguS[11 + g], ("gu", 11 + g), v256)
                for ci in range(2):
                    fc = 2 * g + ci
                    bg, bgR = next_bank()
                    MM(bg[:, 0:n], [(wg[:, kc, ci * 128:(ci + 1) * 128], h2[:, kc, 0:n]) for kc in range(KC)],
                       [wgR, h2R], [bgR])
                    bu, buR = next_bank()
                    MM(bu[:, 0:n], [(wu[:, kc, ci * 128:(ci + 1) * 128], h2[:, kc, 0:n]) for kc in range(KC)],
                       [wuR, h2R], [buR])
                    s_ = srr[0] % 2
                    srr[0] += 1
                    ACTV(sgt[s_][:, 0:n], bg[:, 0:n], AF.Silu, [bgR], [sgtR[s_]])
                    VTT(AT[:, fc, 0:n], sgt[s_][:, 0:n], bu[:, 0:n], ALU.mult, [sgtR[s_], buR], [ATR])
            acc = {}
            k = 0
            for blk in range(nblk):
                for half in range(2):
                    acc[(blk, half)] = bank_k(k)
                    k += 1
            for g in range(11):
                wd, wdR = load_w(WdnS[g], ("dn", g), vdn)
                wdv = vdn(wd)
                for j in range(2):
                    fc = 2 * g + j
                    for blk in range(nblk):
                        for half in range(2):
                            bk, bkR = acc[(blk, half)]
                            MM1(bk[0:bs, :], AT[:, fc, blk * bs:(blk + 1) * bs],
                                wdv[:, j, half * 512:(half + 1) * 512], fc == 0, fc == NFC - 1,
                                [wdR, ATR], [bkR])
            tl = []
            for blk in range(nblk):
                ys = blk
                for half in range(2):
                    bk, bkR = acc[(blk, half)]
                    VTT(yb[ys][0:bs, half * 512:(half + 1) * 512], bk[0:bs, :],
                        x1[blk][0:bs, half * 512:(half + 1) * 512], ALU.add, [bkR, x1R[blk]], [ybR[ys]])
                sl = fe_rr[0] % NFE
                fe_rr[0] += 1
                tl.append((blk, ys, sl, yb[ys][0:bs, :]))
            for (blk, ys, sl, ya) in tl:
                VSTT(hb[sl][0:bs, :], ya, 1.0, ya, ALU.mult, ALU.mult, [ybR[ys]], [hbR[sl], ssR[sl]],
                     accum=ss[sl][0:bs, :])
            for (blk, ys, sl, ya) in tl:
                ACTV(rs[sl][0:bs, :], ss[sl][0:bs, :], AF.Ln, [ssR[sl]], [rsR[sl]],
                     bias=epsmx[0:bs, :], scale=1.0 / D)
            for (blk, ys, sl, ya) in tl:
                ACTV(rs[sl][0:bs, :], rs[sl][0:bs, :], AF.Exp, [rsR[sl]], [rsR[sl]], scale=-0.5)
            for (blk, ys, sl, ya) in tl:
                VSTT(ya, ya, rs[sl][0:bs, 0:1], gfin[0:bs, :], ALU.mult, ALU.mult, [rsR[sl], ybR[ys]], [ybR[ys]])
                DMA(STQ, cfg["y_out"][blk * bs:(blk + 1) * bs, :], ya, [ybR[ys]], [outR], nowaw=True)
            if cfg.get("next") is not None or cfg.get("pf_next"):
                prefetch_w(WinS[6], ("in", 6))
                prefetch_w(WinS[7], ("in", 7))
            bank_rr[0] = 0
            S.barrier()

        def prompt_blocks(i):
            par = i % 2
            blocks = []
            for jb in range(8 * i + 8):
                p, r, bb = jb // 8, (jb % 8) // 4, jb % 4
                b = dict(kcol=jb * 128, vblk=jb, nk=128, pos=jb // 4, q0=0, ops=[], bias=None)
                if p == i - 1 and r == 0 and bb == 3:
                    b["ops"] = [(128, 0, 128, 2 * par)]
                if p == i and r == 0:
                    b["q0"] = 128 * bb
                    b["ops"] = [(0, 128 * bb, min(NT, 128 * bb + 256), None)]
                if p == i and r == 1:
                    b["bias"] = par
                    if bb == 3:
                        b["ops"] = [(128, 0, 128, 2 * par + 1)]
                blocks.append(b)
            return blocks

        nslot_run = NSL if stage >= 1 else 0
        cfgs = [dict(n=NT, bs=128, nblk=4, x=x_own[i], prev=x_prev[i], slot=i, pos=2 * i,
                     kcol=2 * i * 512, vblk=2 * i * 4, k_out=k_own[i], v_out=v_own[i], y_out=y_own[i],
                     tail=(pool_tail, 113) if i == NSLOT - 1 else None) for i in range(nslot_run)]
        for i in range(nslot_run):
            cfg = cfgs[i]
            if stage >= 2 and i + 1 < nslot_run:
                cfg["next"] = cfgs[i + 1]
            phaseB1(cfg)
            if stage >= 2:
                phaseB2(cfg, prompt_blocks(i))
                phaseB3(cfg)

        if stage >= 3:
            arena.reset()
            ckb = arena.take([4, D], BF16)
            ckbR = ckbR_g
            for g4 in range(4):
                DMA("pool", ckb, ck[g4 * 512:(g4 + 1) * 512, :].rearrange("(b p) c -> p b c", p=128), (), [ckbR])
                for blk in range(4):
                    bk, bkR = next_bank()
                    pb = bk.bitcast(BF16)
                    TRS([(pb[:, hh * 128:(hh + 1) * 128], ckb[:, blk, hh * 128:(hh + 1) * 128]) for hh in range(H)],
                        identb[:, :], [ckbR], [bkR])
                    VCOPY(KTst[:, :, blk * 128:(blk + 1) * 128], pb.rearrange("p (k t) -> p k t", k=H),
                          [bkR], [KTstR])
                DMA(STQ, KTs_v[:, :, KT_S + g4 * 512:KT_S + (g4 + 1) * 512], KTst[:], [KTstR], [kvR[16]],
                    nowaw=True)
                DMA("pool", Vst[:], cv[g4 * 512:(g4 + 1) * 512, :].rearrange("(b p) (h e) -> p h b e", p=128, h=H),
                    (), [VstR])
                DMA(STQ, Vs_v[:, :, VB_S + g4 * 4:VB_S + (g4 + 1) * 4, :], Vst[:], [VstR], [kvR[16]],
                    nowaw=True)
            S.barrier()
            cfg = dict(n=NS, bs=NS, nblk=1, x=xs, prev=None, slot=NSLOT, pos=16, kcol=KT_N, vblk=VB_N,
                       k_out=k_s, v_out=v_s, y_out=y_s, tail=(pool_s, 17))
            phaseB1(cfg)
            blocks = []
            for jb in range(16):
                b = dict(kcol=KT_S + jb * 128, vblk=VB_S + jb, nk=128, pos=16, q0=0, ops=[], bias=None)
                if jb == 15:
                    b["ops"] = [(128, 0, NS, None)]
                blocks.append(b)
            blocks.append(dict(kcol=KT_N, vblk=VB_N, nk=NS, pos=16, q0=0, ops=[(0, 0, NS, None)], bias=None))
            phaseB2(cfg, blocks)
            phaseB3(cfg)

        S.add("sp", None, [outR], [])
        S.add("pool", None, [outR], [])

        engsem = {e: es.enter_context(nc.semaphore(f"sem_{e}")) for e in ENGS}
        dsem = {}
        for q in ENGS:
            for i in range(len(S.qs[q])):
                dsem[(q, i)] = es.enter_context(nc.semaphore(f"d_{q}_{i}"))
        cnt = {e: 0 for e in ENGS}
        for op in S.all:
            if not op.dma and op.needed and op.fn is not None:
                cnt[op.eng] += 1
                op.tick = cnt[op.eng]
        blk_ = es.enter_context(nc.Block())
        stats = {e: 0 for e in ENGS}

        def emit(ename, e):
            waited = {}
            for op in S.ops[ename]:
                need = {}
                for d in op.deps:
                    if d.dma:
                        sem, val = dsem[d.key], d.val
                    else:
                        if d.eng == "pe" and ename == "pe":
                            continue
                        if d.fn is None:
                            continue
                        sem, val = engsem[d.eng], d.tick
                    key = sem.num
                    if need.get(key, (None, 0))[1] < val:
                        need[key] = (sem, val)
                for key, (sem, val) in need.items():
                    if waited.get(key, 0) < val:
                        e.wait_ge(sem, val)
                        waited[key] = val
                        stats[ename] += 1
                if op.fn is None:
                    continue
                ins = op.fn(e)
                if op.dma:
                    ins.then_inc(dsem[op.key], 16)
                elif op.needed:
                    ins.then_inc(engsem[ename], 1)

        @blk_.tensor
        def _(e):
            emit("pe", e)

        @blk_.scalar
        def _(e):
            emit("act", e)

        @blk_.vector
        def _(e):
            emit("dve", e)

        @blk_.gpsimd
        def _(e):
            emit("pool", e)

        @blk_.sync
        def _(e):
            emit("sp", e)

        print("ops", {e: len(S.ops[e]) for e in ENGS}, "waits", stats, "sems", len(dsem) + 5)
    return nc


def _own_tile(i, half):
    return 2 * i + (half if i % 2 == 0 else 1 - half)


_CACHE = {}


def kernel(x_prompt, x_sample, cache_k, cache_v, state_pool, rel_bias, norm_mix, w_in, b_gate,
           lambda_q1, lambda_k1, lambda_q2, lambda_k2, subln_g, w_pool, pool_scale, w_out,
           norm_ffn, w_gate_up, w_down, norm_final, _stage=None):
    stage = STAGE if _stage is None else _stage
    f32 = np.float32
    A = lambda a: np.ascontiguousarray(np.asarray(a), dtype=f32)
    x_prompt = A(x_prompt)
    x_sample = A(x_sample)
    cache_k = A(cache_k)
    cache_v = A(cache_v)
    state_pool = A(state_pool)
    B = x_prompt.shape[0]
    xt = x_prompt.reshape(B, 16, NT, D)
    ew = _bucket_consts()
    kk = np.arange(128)[:, None]
    cc = np.arange(256)[None, :]
    maskt = np.where((kk // 64) > (cc // 64), NEG, 0.0).astype(f32)
    jf = np.eye(128, dtype=f32)[::-1].copy()
    common = {
        "w_in": A(w_in)[0], "b_gate": np.ascontiguousarray(A(b_gate)[0].reshape(16, 128).T),
        "w_pool": A(w_pool)[0], "pool_scale": np.ascontiguousarray(A(pool_scale)[0].reshape(8, 128).T),
        "w_out": A(w_out)[0], "w_gu": A(w_gate_up)[0], "w_down": A(w_down)[0],
        "norm_mix": A(norm_mix).reshape(1, D), "norm_ffn": A(norm_ffn).reshape(1, D),
        "norm_final": A(norm_final).reshape(1, D), "subln_g": A(subln_g).reshape(128, 1),
        "lamv": np.concatenate([A(lambda_q1)[0], A(lambda_k1)[0], A(lambda_q2)[0], A(lambda_k2)[0]]).reshape(1, 256),
        "rel_bias": A(rel_bias),
        "ident": np.eye(128, dtype=f32).astype(ml_dtypes.bfloat16), "identf": np.eye(128, dtype=f32),
        "ewin": ew, "jf": jf, "maskt": maskt,
    }
    in_maps = []
    for c in range(8):
        b, half = c // 2, c % 2
        own = [_own_tile(i, half) for i in range(NSLOT)]
        frn = [_own_tile(i, 1 - half) for i in range(NSLOT)]
        x_own = np.ascontiguousarray(xt[b, own])
        x_for = np.ascontiguousarray(xt[b, frn])
        x_prev = np.zeros((NSLOT, 16, D), f32)
        invc = np.zeros((NSLOT + 1, 4, NT), f32)
        for i, t in enumerate(own):
            if t > 0:
                x_prev[i] = x_prompt[b, t * NT - 16:t * NT]
            pos = t * NT + np.arange(NT)
            for g, w in enumerate((2, 4, 8, 16)):
                invc[i, g] = 1.0 / np.minimum(pos + 1, w)
        for g, w in enumerate((2, 4, 8, 16)):
            invc[NSLOT, g] = 1.0 / w
        wsel = np.zeros((128, 4), f32)
        for par in range(2):
            small = (own[par] == 2 * par)
            wsel[:, 2 * par] = 1.0 if small else 0.0
            wsel[:, 2 * par + 1] = 0.0 if small else 1.0
        hist = np.zeros((16, D), f32)
        hist[1:] = state_pool[0, c]
        m = dict(common)
        m.update({"x_own": x_own, "x_for": x_for, "x_prev": x_prev, "wsel": wsel, "invcnt": invc,
                  "xs": x_sample[c], "hist_s": hist, "ck": cache_k[0, c].reshape(PAST, D),
                  "cv": cache_v[0, c].reshape(PAST, D)})
        in_maps.append(m)
    if stage not in _CACHE:
        _CACHE[stage] = build(stage)
    nc = _CACHE[stage]
    res = run_bass_kernel_spmd(nc, in_maps, core_ids=list(range(8)))
    rr = res.results
    y_prompt = np.zeros((B, 16, NT, D), f32)
    k_prompt = np.zeros((B, 16, NT, D), f32)
    v_prompt = np.zeros((B, 16, NT, D), f32)
    pool_prompt = np.zeros((1, B, 15, D), f32)
    y_sample = np.zeros((8, NS, D), f32)
    k_sample = np.zeros((8, NS, D), f32)
    v_sample = np.zeros((8, NS, D), f32)
    pool_sample = np.zeros((1, 8, 15, D), f32)
    for c in range(8):
        b, half = c // 2, c % 2
        own = [_own_tile(i, half) for i in range(NSLOT)]
        r = rr[c]
        y_prompt[b, own] = r["y_own"]
        k_prompt[b, own] = r["k_own"]
        v_prompt[b, own] = r["v_own"]
        if own[-1] == 15:
            pool_prompt[0, b] = r["pool_tail"]
        y_sample[c] = r["y_s"]
        k_sample[c] = r["k_s"]
        v_sample[c] = r["v_s"]
        pool_sample[0, c] = r["pool_s"]
    return (y_prompt.reshape(B, SEQ, D), y_sample,
            k_prompt.reshape(1, B, SEQ, H, 128), v_prompt.reshape(1, B, SEQ, H, 128),
            pool_prompt, k_sample.reshape(1, 8, NS, H, 128), v_sample.reshape(1, 8, NS, H, 128),
            pool_sample)
```

```python
import math
from contextlib import ExitStack
import numpy as np
import ml_dtypes
import concourse.bass as bass
import concourse.mybir as mybir
from concourse.bass_utils import run_bass_kernel_spmd

F32 = mybir.dt.float32
BF16 = mybir.dt.bfloat16
ALU = mybir.AluOpType
AF = mybir.ActivationFunctionType
AP = bass.AP

D = 1024
NT = 512
NSLOT = 8
H = 8
KC = 8
DFF = 2816
NFC = 22
SEQ = 8192
PAST = 2048
NS = 32
NEG = -30000.0
EPS = 1e-6
SUBLN_EPS = 1e-5
LAM_INIT = 0.8 - 0.6 * math.exp(-0.3 * 0)
KT_S = SEQ
KT_N = SEQ + PAST
KTCOLS = SEQ + PAST + 512
VB_S = 64
VB_N = 80
VBLKS = 81
STAGE = 99
STQ = "act"
NFE = 4


class Res:
    __slots__ = ("name", "w", "rs", "sem", "cnt")

    def __init__(self, name):
        self.name = name
        self.w = None
        self.rs = []
        self.sem = {}
        self.cnt = {"hw": 0, "sw": 0}


class Op:
    __slots__ = ("eng", "fn", "deps", "dma", "res", "val", "needed", "tick", "key")

    def __init__(self, eng, fn, dma):
        self.eng = eng
        self.fn = fn
        self.dma = dma
        self.deps = []
        self.res = None
        self.val = 0
        self.needed = False
        self.tick = 0


ENGS = ("pe", "act", "dve", "pool", "sp")
QK = {"sp": 16, "act": 8, "pool": 8, "pe": 1, "dve": 1}


class Sched:
    def __init__(self):
        self.ops = {e: [] for e in ENGS}
        self.all = []
        self.bar = []
        self.dma_since = []
        self.dmares = []
        self.qs = {e: [] for e in ENGS}
        self.qn = {e: 0 for e in ENGS}

    def add(self, eng, fn, reads=(), writes=(), dma=False, nowaw=False, extra=()):
        op = Op(eng, fn, dma)
        deps = set(self.bar)
        deps.update(extra)
        for r in reads:
            if r.w is not None:
                deps.add(r.w)
        for r in writes:
            if r.w is not None and not nowaw:
                deps.add(r.w)
            for q in r.rs:
                deps.add(q)
        for r in reads:
            r.rs.append(op)
        for r in writes:
            if nowaw:
                r.w = op
            else:
                r.w = op
                r.rs = []
        if dma:
            K = QK[eng]
            lst = self.qs[eng]
            i = self.qn[eng] % K
            self.qn[eng] += 1
            if len(lst) <= i:
                lst.append({"cnt": 0, "last": None})
            ent = lst[i]
            if ent["last"] is not None:
                deps.add(ent["last"])
            ent["cnt"] += 16
            ent["last"] = op
            op.key = (eng, i)
            op.val = ent["cnt"]
            self.dma_since.append(op)
        deps.discard(op)
        op.deps = list(deps)
        for d in op.deps:
            d.needed = True
        self.ops[eng].append(op)
        self.all.append(op)
        return op

    def barrier(self):
        b = []
        for e in ENGS:
            if self.ops[e]:
                b.append(self.ops[e][-1])
        b.extend(self.dma_since)
        self.dma_since = []
        self.bar = b


def _bucket_consts():
    import jax
    import jax.numpy as jnp
    cpu = jax.devices("cpu")[0]
    with jax.default_device(cpu):
        rel = jnp.asarray(127 - np.arange(383), dtype=jnp.int32)
        nb = 16
        ret = jnp.where(rel > 0, nb, 0)
        n = jnp.abs(rel)
        max_exact = nb // 2
        large = max_exact + (jnp.log(jnp.maximum(n, 1).astype(jnp.float32) / max_exact)
                             / math.log(128 / max_exact) * (nb - max_exact)).astype(jnp.int32)
        large = jnp.minimum(large, nb - 1)
        bk = np.asarray(ret + jnp.where(n < max_exact, n, large))
    ew = np.zeros((32, 383), np.float32)
    ew[bk, np.arange(383)] += 1.0
    ew[15, :] -= 1.0
    return ew


def build(stage=STAGE):
    nc = bass.Bass("TRN2", target_bir_lowering=False)

    def din(name, shape, dt=F32):
        return nc.dram_tensor(name, list(shape), dt, kind="ExternalInput").ap()

    def dout(name, shape, dt=F32):
        return nc.dram_tensor(name, list(shape), dt, kind="ExternalOutput").ap()

    def dscr(name, shape, dt):
        return nc.dram_tensor(name, list(shape), dt).ap()

    x_own = din("x_own", [NSLOT, NT, D])
    x_for = din("x_for", [NSLOT, NT, D])
    x_prev = din("x_prev", [NSLOT, 16, D])
    wsel = din("wsel", [128, 4])
    invcnt = din("invcnt", [NSLOT + 1, 4, NT])
    xs = din("xs", [NS, D])
    hist_s = din("hist_s", [16, D])
    ck = din("ck", [PAST, D])
    cv = din("cv", [PAST, D])
    w_in = din("w_in", [D, 6 * D])
    b_gate = din("b_gate", [128, 16])
    w_pool = din("w_pool", [4, 256, 256])
    pool_scale = din("pool_scale", [128, 8])
    w_out = din("w_out", [D, D])
    w_gu = din("w_gu", [D, 2 * DFF])
    w_down = din("w_down", [DFF, D])
    norm_mix = din("norm_mix", [1, D])
    norm_ffn = din("norm_ffn", [1, D])
    norm_final = din("norm_final", [1, D])
    subln_g = din("subln_g", [128, 1])
    lamv = din("lamv", [1, 256])
    rel_bias = din("rel_bias", [32, 8])
    ident_d = din("ident", [128, 128], BF16)
    identf_d = din("identf", [128, 128])
    ewin_d = din("ewin", [32, 383])
    jf_d = din("jf", [128, 128])
    maskt_d = din("maskt", [128, 256])

    y_own = dout("y_own", [NSLOT, NT, D])
    k_own = dout("k_own", [NSLOT, NT, D])
    v_own = dout("v_own", [NSLOT, NT, D])
    pool_tail = dout("pool_tail", [15, D])
    y_s = dout("y_s", [NS, D])
    k_s = dout("k_s", [NS, D])
    v_s = dout("v_s", [NS, D])
    pool_s = dout("pool_s", [15, D])

    WinS = dscr("WinS", [12, 128, KC, 512], BF16)
    WoutS = dscr("WoutS", [2, 128, KC, 512], BF16)
    WguS = dscr("WguS", [22, 128, KC, 256], BF16)
    WdnS = dscr("WdnS", [11, 128, 2, D], BF16)
    KTs = dscr("KTs", [H, 128, KTCOLS], BF16)
    Vs = dscr("Vs", [H, 128, VBLKS, 128], BF16)
    gtab = dscr("gtab", [8, 383], F32)
    KTs_v = KTs.rearrange("h d c -> d h c")
    Vs_v = Vs.rearrange("h p b e -> p h b e")

    S = Sched()
    es = ExitStack()
    with es:
        def sb(name, shape, dt=F32):
            return es.enter_context(nc.sbuf_tensor("sb_" + name, list(shape), dt))

        R = Res

        identb = sb("identb", [128, 128], BF16)
        identf = sb("identf", [128, 128])
        onesb = sb("onesb", [128, 128], BF16)
        onesf = sb("onesf", [128, 128])
        gmix = sb("gmix", [128, D])
        gffn = sb("gffn", [128, D])
        gfin = sb("gfin", [128, D])
        bgate = sb("bgate", [128, 16])
        pscale = sb("pscale", [128, 8])
        sg08 = sb("sg08", [128, 1])
        lamt = sb("lamt", [128, 256])
        lamj = sb("lamj", [128, 64])
        lam2 = sb("lam2", [128, 2])
        nlam = sb("nlam", [128, 1])
        cb = sb("cb", [128, 8])
        cbm = sb("cbm", [128, 2, 8])
        negt = sb("negt", [128, 8])
        wselt = sb("wselt", [128, 4])
        Dp = sb("Dp", [128, 8, 256])
        wpool = sb("wpool", [128, 4, 2, 256], BF16)
        epsln = sb("epsln", [128, 1])
        epsmx = sb("epsmx", [128, 1])
        constR = R("const")
        setupR = R("setup")

        xt = [sb(f"xt{i}", [128, D]) for i in range(NFE)]
        xtR = [R(f"xt{i}") for i in range(NFE)]
        hb = [sb(f"hb{i}", [128, D], BF16) for i in range(NFE)]
        hbR = [R(f"hb{i}") for i in range(NFE)]
        ss = [sb(f"ss{i}", [128, 1]) for i in range(NFE)]
        ssR = [R(f"ss{i}") for i in range(NFE)]
        rs = [sb(f"rs{i}", [128, 1]) for i in range(NFE)]
        rsR = [R(f"rs{i}") for i in range(NFE)]
        hTs = [sb(f"hT{i}", [128, KC, NT], BF16) for i in range(2)]
        hTsR = [R(f"hT{i}") for i in range(2)]
        hTp = sb("hTp", [128, KC, 16], BF16)
        hTpR = R("hTp")
        NW = 4
        wst = [sb(f"wst{i}", [128, KC, 512], BF16) for i in range(NW)]
        wstR = [R(f"wst{i}") for i in range(NW)]
        KTst = sb("KTst", [128, H, NT], BF16)
        KTstR = R("KTst")
        Vst = sb("Vst", [128, H, 4, 128], BF16)
        VstR = R("Vst")
        kvo = [sb(f"kvo{i}", [128, 512]) for i in range(2)]
        kvoR = [R(f"kvo{i}") for i in range(2)]
        QT = sb("QT", [128, H, NT], BF16)
        QTR = R("QT")
        PLT = sb("PLT", [128, 8, NT], BF16)
        PLTR = R("PLT")
        MT = sb("MT", [128, 8, NT], BF16)
        MTR = R("MT")
        ARENA_N = 15360
        arena_t = sb("arena", [128, ARENA_N])

        class Arena:
            def __init__(self):
                self.off = 0

            def reset(self):
                self.off = 0

            def take(self, shape, dt=F32):
                n = 1
                for q in shape:
                    n *= q
                nf = n if dt == F32 else (n + 1) // 2
                nf = (nf + 1) // 2 * 2
                assert self.off + nf <= ARENA_N, (self.off, nf)
                a = arena_t[:, self.off:self.off + nf]
                self.off += nf
                if dt != F32:
                    a = a.bitcast(dt)
                a = a[:, 0:n]
                if len(shape) == 2:
                    a = a.rearrange("p (a b) -> p a b", a=shape[0])
                elif len(shape) == 3:
                    a = a.rearrange("p (a b c) -> p a b c", a=shape[0], b=shape[1])
                return a

        arena = Arena()

        PS = [es.enter_context(nc.psum_tensor(f"ps{i}", [128, 2, 512], F32)) for i in range(4)]
        PSR = [[R(f"ps{i}a"), R(f"ps{i}b")] for i in range(4)]
        bank_rr = [0]

        def bank_k(k):
            return PS[k // 2][:, k % 2, :], PSR[k // 2][k % 2]

        def next_bank():
            k = bank_rr[0] % 8
            bank_rr[0] += 1
            return bank_k(k)

        outR = R("outputs")
        invcR_g, hsR_g, ckbR_g = R("invc"), R("hs"), R("ckb")
        KTcR_g = [R(f"KTc{i}") for i in range(4)]
        VcR_g = [R(f"Vc{i}") for i in range(4)]
        x1R_g = [R(f"x1_{i}") for i in range(4)]
        wscrR = R("wscr")
        kvR = [R(f"kvpos{p}") for p in range(17)]
        gtabR = R("gtab")

        pool_dmas = []

        def DMA(q, out, in_, reads, writes, nowaw=False):
            def fn(e):
                return e.dma_start(out=out, in_=in_)
            extra = ()
            if q == "pool" and len(pool_dmas) >= 3:
                extra = (pool_dmas[-3],)
            op = S.add(q, fn, reads, writes, dma=True, nowaw=nowaw, extra=extra)
            if q == "pool":
                pool_dmas.append(op)
            return op

        def MM(out, pairs, reads, writes):
            def fn(e):
                n = len(pairs)
                ins = None
                for i, (l, r) in enumerate(pairs):
                    ins = e.matmul(out, lhsT=l, rhs=r, start=(i == 0), stop=(i == n - 1))
                return ins
            return S.add("pe", fn, reads, writes)

        def MM1(out, l, r, start, stop, reads, writes, skip=False):
            def fn(e):
                if skip:
                    return e.matmul(out, lhsT=l, rhs=r, start=start, stop=stop, skip_group_check=True)
                return e.matmul(out, lhsT=l, rhs=r, start=start, stop=stop)
            return S.add("pe", fn, reads, writes)

        def TRS(items, ident, reads, writes):
            def fn(e):
                ins = None
                for (o, i) in items:
                    ins = e.transpose(o, i, ident)
                return ins
            return S.add("pe", fn, reads, writes)

        def ACTV(out, in_, func, reads, writes, bias=None, scale=None):
            def fn(e):
                kw = {}
                if bias is not None:
                    kw["bias"] = bias
                if scale is not None:
                    kw["scale"] = scale
                return e.activation(out=out, in_=in_, func=func, **kw)
            return S.add("act", fn, reads, writes)

        def VCOPY(out, in_, reads, writes, eng="dve"):
            def fn(e):
                return e.tensor_copy(out=out, in_=in_)
            return S.add(eng, fn, reads, writes)

        def VTT(out, in0, in1, op, reads, writes, eng="dve"):
            def fn(e):
                return e.tensor_tensor(out=out, in0=in0, in1=in1, op=op)
            return S.add(eng, fn, reads, writes)

        def VTS(out, in0, s1, op0, reads, writes, eng="dve"):
            def fn(e):
                return e.tensor_scalar(out=out, in0=in0, scalar1=s1, scalar2=None, op0=op0)
            return S.add(eng, fn, reads, writes)

        def VSTT(out, in0, scalar, in1, op0, op1, reads, writes, accum=None):
            def fn(e):
                if accum is not None:
                    return e.scalar_tensor_tensor(out=out, in0=in0, scalar=scalar, in1=in1,
                                                  op0=op0, op1=op1, accum_out=accum)
                return e.scalar_tensor_tensor(out=out, in0=in0, scalar=scalar, in1=in1,
                                              op0=op0, op1=op1)
            return S.add("dve", fn, reads, writes)

        def VRECIP(out, in_, reads, writes):
            def fn(e):
                return e.reciprocal(out=out, in_=in_)
            return S.add("dve", fn, reads, writes)

        def MEMSET(t, val, writes, eng="pool"):
            def fn(e):
                return e.memset(t, val)
            return S.add(eng, fn, (), writes)

        def bcast_rows(ap2d, nparts):
            n = ap2d.shape[-1]
            return AP(ap2d.tensor, ap2d.offset, [[0, nparts], [1, n]])

        def bcast_mid(a, m):
            apl = a.ap
            return AP(a.tensor, a.offset, [list(apl[0]), [0, m], list(apl[-1])])

        for (t, src) in ((identb[:], ident_d), (identf[:], identf_d),
                         (gmix[:], bcast_rows(norm_mix, 128)), (gffn[:], bcast_rows(norm_ffn, 128)),
                         (gfin[:], bcast_rows(norm_final, 128)), (bgate[:], b_gate),
                         (pscale[:], pool_scale), (sg08[:], subln_g),
                         (lamt[:], bcast_rows(lamv, 128)), (wselt[:], wsel),
                         (cb[:], bcast_rows(rel_bias[15:16, :], 128))):
            DMA("sp", t, src, (), [constR], nowaw=True)
        DMA("pool", wpool[:], w_pool.rearrange("g (cc p) d -> p g cc d", p=128), (), [constR], nowaw=True)
        MEMSET(onesb[:], 1.0, [setupR])
        MEMSET(onesf[:], 1.0, [setupR])
        MEMSET(epsln[:], SUBLN_EPS, [setupR])
        MEMSET(epsmx[:], EPS, [setupR])
        MEMSET(negt[:], NEG, [setupR])
        if True:
            arena.reset()
            ewin = arena.take([384])[0:32, 0:383]
            rbt = arena.take([8])[0:32, :]
            jt = arena.take([128])
            maskt = arena.take([256])
            gsb = arena.take([384])[0:8, 0:383]
            Hk = arena.take([8, 256])
            c2R = R("const2")
            for (t, src) in ((ewin, ewin_d), (rbt, rel_bias), (jt, jf_d), (maskt, maskt_d)):
                DMA("sp", t, src, (), [c2R], nowaw=True)
            S.barrier()
            VTS(sg08[:], sg08[:], 1.0 - LAM_INIT, ALU.mult, [setupR], [setupR])
            VSTT(lamj[:], lamt[:, 0:64], 1.0, lamt[:, 64:128], ALU.mult, ALU.mult, [setupR], [setupR],
                 accum=lam2[:, 0:1])
            VSTT(lamj[:], lamt[:, 128:192], 1.0, lamt[:, 192:256], ALU.mult, ALU.mult, [setupR], [setupR],
                 accum=lam2[:, 1:2])
            ACTV(lam2[:], lam2[:], AF.Exp, [setupR], [setupR])
            VTT(nlam[:], lam2[:, 1:2], lam2[:, 0:1], ALU.subtract, [setupR], [setupR])
            VTS(nlam[:], nlam[:], -LAM_INIT, ALU.add, [setupR], [setupR])
            for par in range(2):
                VSTT(cbm[:, par, :], negt[:], wselt[:, 2 * par:2 * par + 1], cb[:], ALU.mult, ALU.add,
                     [setupR], [setupR])
            bk, bkR = next_bank()
            MM(bk[0:8, 0:383], [(rbt, ewin)], [setupR], [bkR])
            VCOPY(gsb, bk[0:8, 0:383], [bkR], [setupR])
            DMA("sp", gtab, gsb, [setupR], [gtabR])
            hkR = R("Hk")
            DMA("sp", Hk, AP(gtab.tensor, gtab.offset, [[1, 128], [383, 8], [1, 256]]), [gtabR], [hkR])
            for hp in range(4):
                bk, bkR = next_bank()
                for j in range(2):
                    MM(bk[:, j * 256:(j + 1) * 256], [(jt, Hk[:, 2 * hp + j, :])], [hkR, setupR], [bkR])
                VTT(Dp[:, 2 * hp:2 * hp + 2, :], bk.rearrange("p (a b) -> p a b", a=2),
                    bcast_mid(maskt, 2), ALU.add, [bkR, setupR], [setupR])
            S.barrier()

        wrr = [0]

        arena.reset()
        cst = [arena.take([KC, 512], BF16) for _ in range(2)]
        cstR = [R("cst0"), R("cst1")]
        crr_ = [0]
        scrR = {}

        def conv(src, dst, key, view=None):
            sl = crr_[0] % 2
            crr_[0] += 1
            tl = cst[sl] if view is None else view(cst[sl])
            scrR[key] = R("scr")
            DMA("pool", tl, src, (), [cstR[sl]])
            DMA(STQ, dst, tl, [cstR[sl]], [scrR[key]])

        w_in_v = w_in.rearrange("(kc p) c -> p kc c", p=128)
        w_out_v = w_out.rearrange("(kc p) c -> p kc c", p=128)
        w_gu_v = w_gu.rearrange("(kc p) c -> p kc c", p=128)
        w_dn_v = w_down.rearrange("(fc p) c -> p fc c", p=128)

        def v256(t):
            return t[:, :, 0:256]

        def vdn(t):
            return t[:, :, :].rearrange("p a b -> p (a b)")[:, 0:2048].rearrange("p (a b) -> p a b", a=2)

        wkv = [arena.take([KC, 512], BF16) for _ in range(4)]
        wkvR = [R(f"wkv{i}") for i in range(4)]
        if stage >= 0:
            for idx, g in enumerate((2, 3, 4, 5)):
                DMA("pool", wkv[idx], w_in_v[:, :, g * 512:(g + 1) * 512], (), [wkvR[idx]])
                scrR[("in", g)] = R("scr")
                DMA(STQ, WinS[g], wkv[idx], [wkvR[idx]], [scrR[("in", g)]])
        conv_list = []
        for g in (6, 7, 0, 1, 8, 9, 10, 11):
            conv_list.append((w_in_v[:, :, g * 512:(g + 1) * 512], WinS[g], ("in", g), None))
        for g in range(2):
            conv_list.append((w_out_v[:, :, g * 512:(g + 1) * 512], WoutS[g], ("out", g), None))
        for g in range(11):
            conv_list.append((w_gu_v[:, :, g * 256:(g + 1) * 256], WguS[g], ("gu", g), v256))
            conv_list.append((w_gu_v[:, :, (11 + g) * 256:(12 + g) * 256], WguS[11 + g], ("gu", 11 + g), v256))
        for g in range(11):
            conv_list.append((w_dn_v[:, 2 * g:2 * g + 2, :], WdnS[g], ("dn", g), vdn))
        conv_pos = [0]

        def conv_some(k):
            for _ in range(k):
                if conv_pos[0] < len(conv_list):
                    a, b, c, d = conv_list[conv_pos[0]]
                    conv_pos[0] += 1
                    conv(a, b, c, d)


        wrr = [0]

        pfw = {}

        def load_w(src, key, view=None):
            if key in pfw:
                return pfw.pop(key)
            sl = wrr[0] % NW
            wrr[0] += 1
            tl = wst[sl][:] if view is None else view(wst[sl])
            DMA("sp", tl, src, [scrR[key]], [wstR[sl]])
            return wst[sl], wstR[sl]

        def prefetch_w(src, key, view=None):
            pfw[key] = load_w(src, key, view)

        fe_rr = [0]

        def frontend_batch(items):
            st = []
            for it in items:
                sl = fe_rr[0] % NFE
                fe_rr[0] += 1
                n = it["n"]
                if it.get("xsrc") is not None:
                    DMA("sp", xt[sl][0:n, :], it["xsrc"], (), [xtR[sl]])
                    xa, xR = xt[sl][0:n, :], xtR[sl]
                else:
                    xa, xR = it["xtile"], it["xtileR"]
                st.append((sl, n, xa, xR, it))
            for (sl, n, xa, xR, it) in st:
                VSTT(hb[sl][0:n, :], xa, 1.0, xa, ALU.mult, ALU.mult, [xR], [hbR[sl], ssR[sl]],
                     accum=ss[sl][0:n, :])
            for (sl, n, xa, xR, it) in st:
                ACTV(rs[sl][0:n, :], ss[sl][0:n, :], AF.Ln, [ssR[sl]], [rsR[sl]],
                     bias=epsmx[0:n, :], scale=1.0 / D)
            for (sl, n, xa, xR, it) in st:
                ACTV(rs[sl][0:n, :], rs[sl][0:n, :], AF.Exp, [rsR[sl]], [rsR[sl]], scale=-0.5)
            for (sl, n, xa, xR, it) in st:
                VSTT(hb[sl][0:n, :], xa, rs[sl][0:n, 0:1], it["gain"][0:n, :], ALU.mult, ALU.mult,
                     [xR, rsR[sl], hbR[sl]], [hbR[sl]])
            bks = []
            for (sl, n, xa, xR, it) in st:
                bk, bkR = next_bank()
                pb = bk.bitcast(BF16)
                TRS([(pb[:, kc * 128:kc * 128 + n], hb[sl][0:n, kc * 128:(kc + 1) * 128]) for kc in range(KC)],
                    identb[0:n, 0:n], [hbR[sl]], [bkR])
                bks.append((pb, bkR))
            for (sl, n, xa, xR, it), (pb, bkR) in zip(st, bks):
                VCOPY(it["dst"], pb.rearrange("p (k t) -> p k t", k=KC)[:, :, 0:n], [bkR], [it["dstR"]])

        def frontend(n, gain, dst, dstR, xsrc=None, xtile=None, xtileR=None):
            frontend_batch([dict(n=n, gain=gain, dst=dst, dstR=dstR, xsrc=xsrc, xtile=xtile, xtileR=xtileR)])

        kvo_rr = [0]

        def kvo_slot():
            sl = kvo_rr[0] % 2
            kvo_rr[0] += 1
            return kvo[sl], kvoR[sl]

        hT_rr = [0]

        def next_hT():
            sl = hT_rr[0] % 2
            hT_rr[0] += 1
            return hTs[sl], hTsR[sl]

        def kt_fm(hTc, hTcR, n, wt, wR, hbase):
            for ci in range(4):
                bk, bkR = next_bank()
                MM(bk[:, 0:n], [(wt[:, kc, ci * 128:(ci + 1) * 128], hTc[:, kc, 0:n]) for kc in range(KC)],
                   [wR, hTcR], [bkR])
                ACTV(KTst[:, hbase + ci, 0:n], bk[:, 0:n], AF.Copy, [bkR], [KTstR])

        def phaseA_front(f):
            hTc, hTcR = next_hT()
            frontend_batch([dict(n=128, gain=gmix, dst=hTc[:, :, blk * 128:(blk + 1) * 128], dstR=hTcR,
                                 xsrc=x_for[f, blk * 128:(blk + 1) * 128, :]) for blk in range(4)])
            return hTc, hTcR

        def phaseA_mm(f, hTc, hTcR):
            pos = 2 * f + 1
            for gi in (2, 3):
                wt, wR = wkv[gi - 2], wkvR[gi - 2]
                kt_fm(hTc, hTcR, NT, wt, wR, (gi - 2) * 4)
            DMA(STQ, KTs_v[:, :, pos * 512:(pos + 1) * 512], KTst[:], [KTstR], [kvR[pos]], nowaw=True)
            for gi in (4, 5):
                wt, wR = wkv[gi - 2], wkvR[gi - 2]
                for blk in range(4):
                    bk, bkR = next_bank()
                    MM(bk[:, :], [(hTc[:, kc, blk * 128:(blk + 1) * 128], wt[:, kc, :]) for kc in range(KC)],
                       [wR, hTcR], [bkR])
                    VCOPY(Vst[:, (gi - 4) * 4:(gi - 4) * 4 + 4, blk, :],
                          bk.rearrange("p (h e) -> p h e", h=4), [bkR], [VstR])
            DMA(STQ, Vs_v[:, :, pos * 4:(pos + 1) * 4, :], Vst[:], [VstR], [kvR[pos]], nowaw=True)

        NSL = NSLOT
        if stage >= 1:
            cur = phaseA_front(0)
            for f in range(NSL):
                nxt = phaseA_front(f + 1) if f + 1 < NSL else None
                conv_some(6)
                phaseA_mm(f, cur[0], cur[1])
                cur = nxt
        conv_some(len(conv_list))
        S.barrier()

        def phaseB1_front(cfg):
            n, bs, nblk = cfg["n"], cfg["bs"], cfg["nblk"]
            hTc, hTcR = next_hT()
            cfg["hT"] = (hTc, hTcR)
            items = [dict(n=bs, gain=gmix, dst=hTc[:, :, blk * bs:(blk + 1) * bs], dstR=hTcR,
                          xsrc=cfg["x"][blk * bs:(blk + 1) * bs, :]) for blk in range(nblk)]
            frontend_batch(items)
            if cfg["prev"] is not None:
                frontend(16, gmix, hTp[:, :, :], hTpR, xsrc=cfg["prev"])

        def phaseB1(cfg):
            n, bs, nblk = cfg["n"], cfg["bs"], cfg["nblk"]
            arena.reset()
            UT = arena.take([8, 16 + NT])
            invc = arena.take([4, NT])
            a1 = arena.take([2, 16 + NT])
            a2 = arena.take([2, 16 + NT])
            a3 = arena.take([2, 16 + NT])
            DT = arena.take([8, NT], BF16)
            UTR, invcR, aR, DTR = R("UT"), invcR_g, R("apool"), R("DT")
            if "hT" not in cfg:
                phaseB1_front(cfg)
            hTc, hTcR = cfg["hT"]
            DMA("sp", invc, AP(invcnt.tensor, invcnt[cfg["slot"]].offset, [[0, 128], [NT, 4], [1, NT]]),
                (), [invcR])
            if cfg["prev"] is None:
                hs = arena.take([D])
                hsR = hsR_g
                DMA("sp", hs[0:16, :], hist_s, (), [hsR])
                bk, bkR = next_bank()
                TRS([(bk[:, kc * 16:(kc + 1) * 16], hs[0:16, kc * 128:(kc + 1) * 128]) for kc in range(KC)],
                    identf[0:16, 0:16], [hsR], [bkR])
                VCOPY(UT[:, :, 0:16], bk[:, 0:128].rearrange("p (k t) -> p k t", k=KC), [bkR], [UTR])
            for gi in (6, 7):
                wt, wR = load_w(WinS[gi], ("in", gi))
                for ci in range(4):
                    c = (gi - 6) * 4 + ci
                    bk, bkR = next_bank()
                    MM(bk[:, 0:n], [(wt[:, kc, ci * 128:(ci + 1) * 128], hTc[:, kc, 0:n]) for kc in range(KC)],
                       [wR, hTcR], [bkR])
                    VCOPY(UT[:, c, 16:16 + n], bk[:, 0:n], [bkR], [UTR])
                    if cfg["prev"] is not None:
                        bk, bkR = next_bank()
                        MM(bk[:, 0:16], [(wt[:, kc, ci * 128:(ci + 1) * 128], hTp[:, kc, :]) for kc in range(KC)],
                           [wR, hTpR], [bkR])
                        VCOPY(UT[:, c, 0:16], bk[:, 0:16], [bkR], [UTR])
                if cfg["tail"] is not None:
                    tdst, row0 = cfg["tail"]
                    blk = nblk - 1
                    bk, bkR = next_bank()
                    MM(bk[0:bs, :], [(hTc[:, kc, blk * bs:(blk + 1) * bs], wt[:, kc, :]) for kc in range(KC)],
                       [wR, hTcR], [bkR])
                    ko, koR = kvo_slot()
                    VCOPY(ko[0:bs, :], bk[0:bs, :], [bkR], [koR])
                    DMA(STQ, tdst[:, (gi - 6) * 512:(gi - 5) * 512], ko[row0:row0 + 15, :], [koR], [outR],
                        nowaw=True)
            L = 16 + n
            for g in range(4):
                c0 = 2 * g
                U2 = UT[:, c0:c0 + 2, :]
                VTT(a1[:, :, 1:L], U2[:, :, 1:L], U2[:, :, 0:L - 1], ALU.add, [UTR], [aR])
                cur = a1
                if g >= 1:
                    VTT(a2[:, :, 3:L], a1[:, :, 3:L], a1[:, :, 1:L - 2], ALU.add, [aR], [aR])
                    cur = a2
                if g >= 2:
                    VTT(a3[:, :, 7:L], a2[:, :, 7:L], a2[:, :, 3:L - 4], ALU.add, [aR], [aR])
                    cur = a3
                if g >= 3:
                    VTT(a1[:, :, 15:L], a3[:, :, 15:L], a3[:, :, 7:L - 8], ALU.add, [aR], [aR])
                    cur = a1
                oth = a2 if cur is not a2 else a3
                VTT(oth[:, :, 16:L], cur[:, :, 16:L], bcast_mid(invc[:, g, 0:n], 2), ALU.mult,
                    [aR, invcR], [aR])
                VTT(DT[:, c0:c0 + 2, 0:n], oth[:, :, 16:L], U2[:, :, 16:L], ALU.subtract, [aR, UTR], [DTR])
            for gi in (0, 1):
                wt, wR = load_w(WinS[gi], ("in", gi))
                for ci in range(4):
                    bk, bkR = next_bank()
                    MM(bk[:, 0:n], [(wt[:, kc, ci * 128:(ci + 1) * 128], hTc[:, kc, 0:n]) for kc in range(KC)],
                       [wR, hTcR], [bkR])
                    ACTV(QT[:, gi * 4 + ci, 0:n], bk[:, 0:n], AF.Copy, [bkR], [QTR], scale=0.125)
            for gi in (2, 3):
                wt, wR = load_w(WinS[gi], ("in", gi))
                kt_fm(hTc, hTcR, n, wt, wR, (gi - 2) * 4)
                for blk in range(nblk):
                    bk, bkR = next_bank()
                    MM(bk[0:bs, :], [(hTc[:, kc, blk * bs:(blk + 1) * bs], wt[:, kc, :]) for kc in range(KC)],
                       [wR, hTcR], [bkR])
                    ko, koR = kvo_slot()
                    VCOPY(ko[0:bs, :], bk[0:bs, :], [bkR], [koR])
                    DMA(STQ, cfg["k_out"][blk * bs:(blk + 1) * bs, (gi - 2) * 512:(gi - 1) * 512],
                        ko[0:bs, :], [koR], [outR], nowaw=True)
            DMA(STQ, KTs_v[:, :, cfg["kcol"]:cfg["kcol"] + n], KTst[:, :, 0:n], [KTstR],
                [kvR[cfg["pos"]]], nowaw=True)
            for gi in (4, 5):
                wt, wR = load_w(WinS[gi], ("in", gi))
                for blk in range(nblk):
                    bk, bkR = next_bank()
                    MM(bk[0:bs, :], [(hTc[:, kc, blk * bs:(blk + 1) * bs], wt[:, kc, :]) for kc in range(KC)],
                       [wR, hTcR], [bkR])
                    ko, koR = kvo_slot()
                    VCOPY(ko[0:bs, :], bk[0:bs, :], [bkR], [koR])
                    VCOPY(Vst[0:bs, (gi - 4) * 4:(gi - 4) * 4 + 4, blk, :],
                          ko[0:bs, :].rearrange("p (h e) -> p h e", h=4), [koR], [VstR], eng="pool")
                    DMA(STQ, cfg["v_out"][blk * bs:(blk + 1) * bs, (gi - 4) * 512:(gi - 3) * 512],
                        ko[0:bs, :], [koR], [outR], nowaw=True)
            DMA(STQ, Vs_v[0:bs, :, cfg["vblk"]:cfg["vblk"] + nblk, :], Vst[0:bs, :, 0:nblk, :], [VstR],
                [kvR[cfg["pos"]]], nowaw=True)
            for g in range(4):
                for oc in range(2):
                    bk, bkR = next_bank()
                    MM(bk[:, 0:n], [(wpool[:, g, cc, oc * 128:(oc + 1) * 128], DT[:, 2 * g + cc, 0:n])
                                    for cc in range(2)], [DTR], [bkR])
                    c = 2 * g + oc
                    VTS(PLT[:, c, 0:n], bk[:, 0:n], pscale[:, c:c + 1], ALU.mult, [bkR], [PLTR])
            S.barrier()

        def phaseB2(cfg, blocks):
            n = cfg["n"]
            arena.reset()
            NCH = 4
            KTc = [arena.take([2048], BF16) for _ in range(NCH)]
            Vc = [arena.take([16, 128], BF16) for _ in range(NCH)]
            KTcR = KTcR_g
            VcR = VcR_g
            NPT = 4
            PT = [arena.take([2, NT], BF16) for _ in range(NPT)]
            PTR = [R(f"PT{i}") for i in range(NPT)]
            rr_ = arena.take([2, NT])
            AA = arena.take([2, NT])
            Lacc = arena.take([2, NT])
            LaccR = [R("Lacc0"), R("Lacc1")]
            Dd = arena.take([NT])
            sq = arena.take([NT], BF16)
            rstd = arena.take([NT])
            epR = R("ep")
            Sps = [PS[0], PS[1]]
            SpsR = [R("S0"), R("S1")]
            Ops, OpsR = PS[2], R("Ops")
            Lps = PS[3]
            Lps0R, Lps1R = R("Lps0"), R("Lps1")
            pend = [None]
            chunks = [blocks[i:i + 16] for i in range(0, len(blocks), 16)]
            crr = [0]
            srr = [0]
            for h in range(H):
                loaded = []
                for ch in chunks:
                    sl = crr[0] % NCH
                    crr[0] += 1
                    ncols = sum(b["nk"] for b in ch)
                    kc0 = ch[0]["kcol"]
                    vb0 = ch[0]["vblk"]
                    rd = [kvR[p] for p in sorted(set(b["pos"] for b in ch))]
                    DMA("sp", KTc[sl][:, 0:ncols], KTs[h, :, kc0:kc0 + ncols], rd, [KTcR[sl]])
                    pr = ch[0]["nk"] if len(ch) == 1 else 128
                    DMA("sp", Vc[sl][0:pr, 0:len(ch), :], Vs[h, 0:pr, vb0:vb0 + len(ch), :], rd, [VcR[sl]])
                    loaded.append(sl)
                flat = []
                for ci, ch in enumerate(chunks):
                    off = 0
                    for bi, b in enumerate(ch):
                        flat.append((loaded[ci], off, bi, b))
                        off += b["nk"]
                nb = len(flat)

                def emit_qk(j):
                    sl, off, bi, b = flat[j]
                    si = srr[0] % 2
                    srr[0] += 1
                    nk, q0 = b["nk"], b["q0"]
                    for m in range(2):
                        MM1(Sps[si][0:nk, m, q0:n], KTc[sl][64 * m:64 * m + 64, off:off + nk],
                            QT[64 * m:64 * m + 64, h, q0:n], True, True, [KTcR[sl], QTR], [SpsR[si]])
                    for (dc, qa, qb, wap) in b["ops"]:
                        sv = Sps[si][0:nk, :, qa:qb]
                        dv = bcast_mid(Dp[0:nk, h, dc:dc + (qb - qa)], 2)
                        if wap is None:
                            VTT(sv, sv, dv, ALU.add, [SpsR[si]], [SpsR[si]])
                        else:
                            VSTT(sv, dv, wselt[0:nk, wap:wap + 1], sv, ALU.mult, ALU.add,
                                 [SpsR[si]], [SpsR[si]])
                    return si

                sidx = {}
                sidx[0] = emit_qk(0)
                if nb > 1:
                    sidx[1] = emit_qk(1)
                if pend[0] is not None:
                    pend[0][0]()
                for j in range(nb):
                    sl, off, bi, b = flat[j]
                    nk, q0 = b["nk"], b["q0"]
                    si = sidx[j]
                    pi = j % NPT
                    bias_ap = cb[0:nk, h:h + 1] if b["bias"] is None else cbm[0:nk, b["bias"], h:h + 1]
                    ACTV(PT[pi][0:nk, :, q0:n], Sps[si][0:nk, :, q0:n], AF.Exp, [SpsR[si]], [PTR[pi]],
                         bias=bias_ap)
                    if j + 2 < nb:
                        sidx[j + 2] = emit_qk(j + 2)
                    for m in range(2):
                        MM1(Ops[:, m, q0:n], Vc[sl][0:nk, bi, :], PT[pi][0:nk, m, q0:n],
                            j == 0, j == nb - 1, [VcR[sl], PTR[pi]], [OpsR])
                    for m in range(2):
                        eng = "dve" if m == 0 else "pool"
                        if m == 1:
                            MM1(Lps[:, 1, q0:n], onesb[0:nk, :], PT[pi][0:nk, 1, q0:n],
                                j == 0, j == nb - 1, [PTR[pi]], [Lps1R])
                        elif j == 0:
                            VCOPY(Lacc[0:nk, m, q0:n], PT[pi][0:nk, m, q0:n], [PTR[pi]], [LaccR[m]], eng=eng)
                        else:
                            VTT(Lacc[0:nk, m, q0:n], Lacc[0:nk, m, q0:n], PT[pi][0:nk, m, q0:n], ALU.add,
                                [PTR[pi], LaccR[m]], [LaccR[m]], eng=eng)
                    if j == 1 and pend[0] is not None:
                        pend[0][1]()
                        pend[0] = None

                def make_ep(h, nb):
                    def ep1():
                        MM1(Lps[:, 0, 0:n], onesf[:, :], Lacc[:, 0, 0:n], True, True, [LaccR[0]], [Lps0R])
                        VRECIP(rr_[:, :, 0:n], Lps[:, :, 0:n], [Lps0R, Lps1R], [epR])
                        VTT(AA[:, :, 0:n], Ops[:, :, 0:n], rr_[:, :, 0:n], ALU.mult, [OpsR, epR], [epR])

                    def ep2():
                        VSTT(Dd[:, 0:n], AA[:, 1, 0:n], nlam[:, 0:1], AA[:, 0, 0:n], ALU.mult, ALU.add,
                             [epR], [epR])
                        VTT(sq[:, 0:n], Dd[:, 0:n], Dd[:, 0:n], ALU.mult, [epR], [epR])
                        MM1(Lps[:, 0, 0:n], onesb[:, :], sq[:, 0:n], True, True, [epR], [Lps0R])
                        ACTV(rstd[:, 0:n], Lps[:, 0, 0:n], AF.Ln, [Lps0R], [epR], bias=epsln[:, :],
                             scale=1.0 / 128)
                        ACTV(rstd[:, 0:n], rstd[:, 0:n], AF.Exp, [epR], [epR], scale=-0.5)
                        VSTT(MT[:, h, 0:n], Dd[:, 0:n], sg08[:, 0:1], rstd[:, 0:n], ALU.mult, ALU.mult,
                             [epR], [MTR])
                    return (ep1, ep2)

                pend[0] = make_ep(h, nb)
            pend[0][0]()
            pend[0][1]()
            pend[0] = None
            prefetch_w(WinS[8], ("in", 8))
            prefetch_w(WinS[9], ("in", 9))
            S.barrier()

        def phaseB3(cfg):
            n, bs, nblk = cfg["n"], cfg["bs"], cfg["nblk"]
            hTc, hTcR = cfg["hT"]
            arena.reset()
            MTf = arena.take([8, NT])
            gtm = [arena.take([NT]) for _ in range(2)]
            MTfR, gtmR = R("MTf"), [R("gt0"), R("gt1")]
            grr = [0]
            for gi in (8, 9, 10, 11):
                wt, wR = load_w(WinS[gi], ("in", gi))
                for ci in range(4):
                    c = ((gi - 8) % 2) * 4 + ci
                    bk, bkR = next_bank()
                    MM(bk[:, 0:n], [(wt[:, kc, ci * 128:(ci + 1) * 128], hTc[:, kc, 0:n]) for kc in range(KC)],
                       [wR, hTcR], [bkR])
                    gs = grr[0] % 2
                    grr[0] += 1
                    col = (gi - 8) * 4 + ci
                    ACTV(gtm[gs][:, 0:n], bk[:, 0:n], AF.Sigmoid, [bkR], [gtmR[gs]], bias=bgate[:, col:col + 1])
                    if gi < 10:
                        VTT(MTf[:, c, 0:n], gtm[gs][:, 0:n], MT[:, c, 0:n], ALU.mult, [gtmR[gs], MTR], [MTfR])
                    else:
                        VTT(gtm[gs][:, 0:n], gtm[gs][:, 0:n], PLT[:, c, 0:n], ALU.mult, [gtmR[gs], PLTR],
                            [gtmR[gs]])
                        VTT(MT[:, c, 0:n], gtm[gs][:, 0:n], MTf[:, c, 0:n], ALU.add, [gtmR[gs], MTfR], [MTR])
            prefetch_w(WoutS[0], ("out", 0))
            prefetch_w(WoutS[1], ("out", 1))
            S.barrier()
            arena.reset()
            x1 = [arena.take([D]) for _ in range(nblk)]
            x1R = x1R_g[:nblk]
            AT = arena.take([NFC, NT], BF16)
            ATR = R("AT")
            yb = [arena.take([D]) for _ in range(nblk)]
            ybR = [R(f"yb{i}") for i in range(nblk)]
            sgt = [arena.take([NT]) for _ in range(2)]
            sgtR = [R("sg0"), R("sg1")]
            w0, w0R = load_w(WoutS[0], ("out", 0))
            w1, w1R = load_w(WoutS[1], ("out", 1))
            h2, h2R = next_hT()
            for blk in range(nblk):
                DMA("sp", x1[blk][0:bs, :], cfg["x"][blk * bs:(blk + 1) * bs, :], (), [x1R[blk]])
            for blk in range(nblk):
                for half, (wt, wR) in enumerate(((w0, w0R), (w1, w1R))):
                    bk, bkR = next_bank()
                    MM(bk[0:bs, :], [(MT[:, kc, blk * bs:(blk + 1) * bs], wt[:, kc, :]) for kc in range(KC)],
                       [wR, MTR], [bkR])
                    VTT(x1[blk][0:bs, half * 512:(half + 1) * 512], bk[0:bs, :],
                        x1[blk][0:bs, half * 512:(half + 1) * 512], ALU.add, [bkR, x1R[blk]], [x1R[blk]])
            frontend_batch([dict(n=bs, gain=gffn, dst=h2[:, :, blk * bs:(blk + 1) * bs], dstR=h2R,
                                 xtile=x1[blk][0:bs, :], xtileR=x1R[blk]) for blk in range(nblk)])
            srr = [0]
            for g in range(11):
                if g == 2 and cfg.get("next") is not None:
                    phaseB1_front(cfg["next"])
                wg, wgR = load_w(WguS[g], ("gu", g), v256)
                wu, wuR = load_w(WguS[11 + g], ("gu", 11 + g), v256)
                for ci in range(2):
                    fc = 2 * g + ci
                    bg, bgR = next_bank()
                    MM(bg[:, 0:n], [(wg[:, kc, ci * 128:(ci + 1) * 128], h2[:, kc, 0:n]) for kc in range(KC)],
                       [wgR, h2R], [bgR])
                    bu, buR = next_bank()
                    MM(bu[:, 0:n], [(wu[:, kc, ci * 128:(ci + 1) * 128], h2[:, kc, 0:n]) for kc in range(KC)],
                       [wuR, h2R], [buR])
                    s_ = srr[0] % 2
                    srr[0] += 1
                    ACTV(sgt[s_][:, 0:n], bg[:, 0:n], AF.Silu, [bgR], [sgtR[s_]])
                    VTT(AT[:, fc, 0:n], sgt[s_][:, 0:n], bu[:, 0:n], ALU.mult, [sgtR[s_], buR], [ATR])
            acc = {}
            k = 0
            for blk in range(nblk):
                for half in range(2):
                    acc[(blk, half)] = bank_k(k)
                    k += 1
            for g in range(11):
                wd, wdR = load_w(WdnS[g], ("dn", g), vdn)
                wdv = vdn(wd)
                for j in range(2):
                    fc = 2 * g + j
                    for blk in range(nblk):
                        for half in range(2):
                            bk, bkR = acc[(blk, half)]
                            MM1(bk[0:bs, :], AT[:, fc, blk * bs:(blk + 1) * bs],
                                wdv[:, j, half * 512:(half + 1) * 512], fc == 0, fc == NFC - 1,
                                [wdR, ATR], [bkR])
            tl = []
            for blk in range(nblk):
                ys = blk
                for half in range(2):
                    bk, bkR = acc[(blk, half)]
                    VTT(yb[ys][0:bs, half * 512:(half + 1) * 512], bk[0:bs, :],
                        x1[blk][0:bs, half * 512:(half + 1) * 512], ALU.add, [bkR, x1R[blk]], [ybR[ys]])
                sl = fe_rr[0] % NFE
                fe_rr[0] += 1
                tl.append((blk, ys, sl, yb[ys][0:bs, :]))
            for (blk, ys, sl, ya) in tl:
                VSTT(hb[sl][0:bs, :], ya, 1.0, ya, ALU.mult, ALU.mult, [ybR[ys]], [hbR[sl], ssR[sl]],
                     accum=ss[sl][0:bs, :])
            for (blk, ys, sl, ya) in tl:
                ACTV(rs[sl][0:bs, :], ss[sl][0:bs, :], AF.Ln, [ssR[sl]], [rsR[sl]],
                     bias=epsmx[0:bs, :], scale=1.0 / D)
            for (blk, ys, sl, ya) in tl:
                ACTV(rs[sl][0:bs, :], rs[sl][0:bs, :], AF.Exp, [rsR[sl]], [rsR[sl]], scale=-0.5)
            for (blk, ys, sl, ya) in tl:
                VSTT(ya, ya, rs[sl][0:bs, 0:1], gfin[0:bs, :], ALU.mult, ALU.mult, [rsR[sl], ybR[ys]], [ybR[ys]])
                DMA(STQ, cfg["y_out"][blk * bs:(blk + 1) * bs, :], ya, [ybR[ys]], [outR], nowaw=True)
            if cfg.get("next") is not None or cfg.get("pf_next"):
                prefetch_w(WinS[6], ("in", 6))
                prefetch_w(WinS[7], ("in", 7))
            bank_rr[0] = 0
            S.barrier()

        def prompt_blocks(i):
            par = i % 2
            blocks = []
            for jb in range(8 * i + 8):
                p, r, bb = jb // 8, (jb % 8) // 4, jb % 4
                b = dict(kcol=jb * 128, vblk=jb, nk=128, pos=jb // 4, q0=0, ops=[], bias=None)
                if p == i - 1 and r == 0 and bb == 3:
                    b["ops"] = [(128, 0, 128, 2 * par)]
                if p == i and r == 0:
                    b["q0"] = 128 * bb
                    b["ops"] = [(0, 128 * bb, min(NT, 128 * bb + 256), None)]
                if p == i and r == 1:
                    b["bias"] = par
                    if bb == 3:
                        b["ops"] = [(128, 0, 128, 2 * par + 1)]
                blocks.append(b)
            return blocks

        nslot_run = NSL if stage >= 1 else 0
        cfgs = [dict(n=NT, bs=128, nblk=4, x=x_own[i], prev=x_prev[i], slot=i, pos=2 * i,
                     kcol=2 * i * 512, vblk=2 * i * 4, k_out=k_own[i], v_out=v_own[i], y_out=y_own[i],
                     tail=(pool_tail, 113) if i == NSLOT - 1 else None) for i in range(nslot_run)]
        for i in range(nslot_run):
            cfg = cfgs[i]
            if stage >= 2 and i + 1 < nslot_run:
                cfg["next"] = cfgs[i + 1]
            phaseB1(cfg)
            if stage >= 2:
                phaseB2(cfg, prompt_blocks(i))
                phaseB3(cfg)

        if stage >= 3:
            arena.reset()
            ckb = arena.take([4, D], BF16)
            ckbR = ckbR_g
            for g4 in range(4):
                DMA("pool", ckb, ck[g4 * 512:(g4 + 1) * 512, :].rearrange("(b p) c -> p b c", p=128), (), [ckbR])
                for blk in range(4):
                    bk, bkR = next_bank()
                    pb = bk.bitcast(BF16)
                    TRS([(pb[:, hh * 128:(hh + 1) * 128], ckb[:, blk, hh * 128:(hh + 1) * 128]) for hh in range(H)],
                        identb[:, :], [ckbR], [bkR])
                    VCOPY(KTst[:, :, blk * 128:(blk + 1) * 128], pb.rearrange("p (k t) -> p k t", k=H),
                          [bkR], [KTstR])
                DMA(STQ, KTs_v[:, :, KT_S + g4 * 512:KT_S + (g4 + 1) * 512], KTst[:], [KTstR], [kvR[16]],
                    nowaw=True)
                DMA("pool", Vst[:], cv[g4 * 512:(g4 + 1) * 512, :].rearrange("(b p) (h e) -> p h b e", p=128, h=H),
                    (), [VstR])
                DMA(STQ, Vs_v[:, :, VB_S + g4 * 4:VB_S + (g4 + 1) * 4, :], Vst[:], [VstR], [kvR[16]],
                    nowaw=True)
            S.barrier()
            cfg = dict(n=NS, bs=NS, nblk=1, x=xs, prev=None, slot=NSLOT, pos=16, kcol=KT_N, vblk=VB_N,
                       k_out=k_s, v_out=v_s, y_out=y_s, tail=(pool_s, 17))
            phaseB1(cfg)
            blocks = []
            for jb in range(16):
                b = dict(kcol=KT_S + jb * 128, vblk=VB_S + jb, nk=128, pos=16, q0=0, ops=[], bias=None)
                if jb == 15:
                    b["ops"] = [(128, 0, NS, None)]
                blocks.append(b)
            blocks.append(dict(kcol=KT_N, vblk=VB_N, nk=NS, pos=16, q0=0, ops=[(0, 0, NS, None)], bias=None))
            phaseB2(cfg, blocks)
            phaseB3(cfg)

        S.add("sp", None, [outR], [])
        S.add("pool", None, [outR], [])

        engsem = {e: es.enter_context(nc.semaphore(f"sem_{e}")) for e in ENGS}
        dsem = {}
        for q in ENGS:
            for i in range(len(S.qs[q])):
                dsem[(q, i)] = es.enter_context(nc.semaphore(f"d_{q}_{i}"))
        cnt = {e: 0 for e in ENGS}
        for op in S.all:
            if not op.dma and op.needed and op.fn is not None:
                cnt[op.eng] += 1
                op.tick = cnt[op.eng]
        blk_ = es.enter_context(nc.Block())
        stats = {e: 0 for e in ENGS}

        def emit(ename, e):
            waited = {}
            for op in S.ops[ename]:
                need = {}
                for d in op.deps:
                    if d.dma:
                        sem, val = dsem[d.key], d.val
                    else:
                        if d.eng == "pe" and ename == "pe":
                            continue
                        if d.fn is None:
                            continue
                        sem, val = engsem[d.eng], d.tick
                    key = sem.num
                    if need.get(key, (None, 0))[1] < val:
                        need[key] = (sem, val)
                for key, (sem, val) in need.items():
                    if waited.get(key, 0) < val:
                        e.wait_ge(sem, val)
                        waited[key] = val
                        stats[ename] += 1
                if op.fn is None:
                    continue
                ins = op.fn(e)
                if op.dma:
                    ins.then_inc(dsem[op.key], 16)
                elif op.needed:
                    ins.then_inc(engsem[ename], 1)

        @blk_.tensor
        def _(e):
            emit("pe", e)

        @blk_.scalar
        def _(e):
            emit("act", e)

        @blk_.vector
        def _(e):
            emit("dve", e)

        @blk_.gpsimd
        def _(e):
            emit("pool", e)

        @blk_.sync
        def _(e):
            emit("sp", e)

        print("ops", {e: len(S.ops[e]) for e in ENGS}, "waits", stats, "sems", len(dsem) + 5)
    return nc


def _own_tile(i, half):
    return 2 * i + (half if i % 2 == 0 else 1 - half)


_CACHE = {}


def kernel(x_prompt, x_sample, cache_k, cache_v, state_pool, rel_bias, norm_mix, w_in, b_gate,
           lambda_q1, lambda_k1, lambda_q2, lambda_k2, subln_g, w_pool, pool_scale, w_out,
           norm_ffn, w_gate_up, w_down, norm_final, _stage=None):
    stage = STAGE if _stage is None else _stage
    f32 = np.float32
    A = lambda a: np.ascontiguousarray(np.asarray(a), dtype=f32)
    x_prompt = A(x_prompt)
    x_sample = A(x_sample)
    cache_k = A(cache_k)
    cache_v = A(cache_v)
    state_pool = A(state_pool)
    B = x_prompt.shape[0]
    xt = x_prompt.reshape(B, 16, NT, D)
    ew = _bucket_consts()
    kk = np.arange(128)[:, None]
    cc = np.arange(256)[None, :]
    maskt = np.where((kk // 64) > (cc // 64), NEG, 0.0).astype(f32)
    jf = np.eye(128, dtype=f32)[::-1].copy()
    common = {
        "w_in": A(w_in)[0], "b_gate": np.ascontiguousarray(A(b_gate)[0].reshape(16, 128).T),
        "w_pool": A(w_pool)[0], "pool_scale": np.ascontiguousarray(A(pool_scale)[0].reshape(8, 128).T),
        "w_out": A(w_out)[0], "w_gu": A(w_gate_up)[0], "w_down": A(w_down)[0],
        "norm_mix": A(norm_mix).reshape(1, D), "norm_ffn": A(norm_ffn).reshape(1, D),
        "norm_final": A(norm_final).reshape(1, D), "subln_g": A(subln_g).reshape(128, 1),
        "lamv": np.concatenate([A(lambda_q1)[0], A(lambda_k1)[0], A(lambda_q2)[0], A(lambda_k2)[0]]).reshape(1, 256),
        "rel_bias": A(rel_bias),
        "ident": np.eye(128, dtype=f32).astype(ml_dtypes.bfloat16), "identf": np.eye(128, dtype=f32),
        "ewin": ew, "jf": jf, "maskt": maskt,
    }
    in_maps = []
    for c in range(8):
        b, half = c // 2, c % 2
        own = [_own_tile(i, half) for i in range(NSLOT)]
        frn = [_own_tile(i, 1 - half) for i in range(NSLOT)]
        x_own = np.ascontiguousarray(xt[b, own])
        x_for = np.ascontiguousarray(xt[b, frn])
        x_prev = np.zeros((NSLOT, 16, D), f32)
        invc = np.zeros((NSLOT + 1, 4, NT), f32)
        for i, t in enumerate(own):
            if t > 0:
                x_prev[i] = x_prompt[b, t * NT - 16:t * NT]
            pos = t * NT + np.arange(NT)
            for g, w in enumerate((2, 4, 8, 16)):
                invc[i, g] = 1.0 / np.minimum(pos + 1, w)
        for g, w in enumerate((2, 4, 8, 16)):
            invc[NSLOT, g] = 1.0 / w
        wsel = np.zeros((128, 4), f32)
        for par in range(2):
            small = (own[par] == 2 * par)
            wsel[:, 2 * par] = 1.0 if small else 0.0
            wsel[:, 2 * par + 1] = 0.0 if small else 1.0
        hist = np.zeros((16, D), f32)
        hist[1:] = state_pool[0, c]
        m = dict(common)
        m.update({"x_own": x_own, "x_for": x_for, "x_prev": x_prev, "wsel": wsel, "invcnt": invc,
                  "xs": x_sample[c], "hist_s": hist, "ck": cache_k[0, c].reshape(PAST, D),
                  "cv": cache_v[0, c].reshape(PAST, D)})
        in_maps.append(m)
    if stage not in _CACHE:
        _CACHE[stage] = build(stage)
    nc = _CACHE[stage]
    res = run_bass_kernel_spmd(nc, in_maps, core_ids=list(range(8)))
    rr = res.results
    y_prompt = np.zeros((B, 16, NT, D), f32)
    k_prompt = np.zeros((B, 16, NT, D), f32)
    v_prompt = np.zeros((B, 16, NT, D), f32)
    pool_prompt = np.zeros((1, B, 15, D), f32)
    y_sample = np.zeros((8, NS, D), f32)
    k_sample = np.zeros((8, NS, D), f32)
    v_sample = np.zeros((8, NS, D), f32)
    pool_sample = np.zeros((1, 8, 15, D), f32)
    for c in range(8):
        b, half = c // 2, c % 2
        own = [_own_tile(i, half) for i in range(NSLOT)]
        r = rr[c]
        y_prompt[b, own] = r["y_own"]
        k_prompt[b, own] = r["k_own"]
        v_prompt[b, own] = r["v_own"]
        if own[-1] == 15:
            pool_prompt[0, b] = r["pool_tail"]
        y_sample[c] = r["y_s"]
        k_sample[c] = r["k_s"]
        v_sample[c] = r["v_s"]
        pool_sample[0, c] = r["pool_s"]
    return (y_prompt.reshape(B, SEQ, D), y_sample,
            k_prompt.reshape(1, B, SEQ, H, 128), v_prompt.reshape(1, B, SEQ, H, 128),
            pool_prompt, k_sample.reshape(1, 8, NS, H, 128), v_sample.reshape(1, 8, NS, H, 128),
            pool_sample)
```
